# Optimizing a Trainium2 kernel written in Bass

```python
import jax, jax.numpy as jnp
from jax import lax
import numpy as np

D_MODEL = 4096
BATCH = 2
SEQ = 8192
DEPTH = 2

N_MIXERS = 2
N_POOL_LAYERS = (DEPTH + 1) // 2
N_ATTN_LAYERS = DEPTH // 2
SELF_W = 3 * D_MODEL // 4
MEM_LEN = 256
XA_HEADS = 4
XA_W = D_MODEL // 4
XA_HEAD_DIM = XA_W // XA_HEADS
POOL_WINDOWS = (2, 4, 8, 16)
N_POOL_GROUPS = len(POOL_WINDOWS)
POOL_GROUP = SELF_W // N_POOL_GROUPS
HEAD_DIM = 128
N_Q_HEADS = SELF_W // HEAD_DIM
GQA_GROUP = 8
N_KV_HEADS = N_Q_HEADS // GQA_GROUP
KV_W = N_KV_HEADS * HEAD_DIM
WINDOW = 128
BLOCK = WINDOW
ROT_DIM = HEAD_DIM // 4
ROPE_THETA = 500000.0
D_FF = 4 * D_MODEL
EPS = 1e-6
NEG = -1e30

kernel_name = "hybrid_pool_swa_memxattn_sqrelu"


def rms_norm(x, g):
    xf = x.astype(jnp.float32)
    y = xf * lax.rsqrt(jnp.mean(xf * xf, axis=-1, keepdims=True) + EPS)
    return (y * g.astype(jnp.float32)).astype(x.dtype)


def rope_tables(positions):
    inv_freq = ROPE_THETA ** (-jnp.arange(0, ROT_DIM, 2, dtype=jnp.float32) / ROT_DIM)
    ang = positions.astype(jnp.float32)[..., None] * inv_freq
    return jnp.cos(ang)[:, :, None, :], jnp.sin(ang)[:, :, None, :]


def partial_rope(x, cos, sin):
    xf = x.astype(jnp.float32)
    x1 = xf[..., : ROT_DIM // 2]
    x2 = xf[..., ROT_DIM // 2: ROT_DIM]
    rot = jnp.concatenate([x1 * cos - x2 * sin, x2 * cos + x1 * sin], axis=-1)
    return jnp.concatenate([rot, xf[..., ROT_DIM:]], axis=-1).astype(x.dtype)


def causal_pool_mixer(u, w_group, scale):
    B, S, _ = u.shape
    ug = u.reshape(B, S, N_POOL_GROUPS, POOL_GROUP)
    c0 = jnp.pad(jnp.cumsum(ug.astype(jnp.float32), axis=1), ((0, 0), (1, 0), (0, 0), (0, 0)))
    t1 = jnp.arange(1, S + 1, dtype=jnp.float32)
    means = []
    for g, w in enumerate(POOL_WINDOWS):
        cg = c0[:, :, g]
        lo = jnp.concatenate([jnp.zeros_like(cg[:, : w - 1]), cg[:, : S + 1 - w]], axis=1)
        means.append((cg[:, 1:] - lo) / jnp.minimum(t1, float(w))[None, :, None])
    pooled = jnp.stack(means, axis=2)
    p = (pooled - ug.astype(jnp.float32)).astype(u.dtype)
    y = jnp.einsum('bsgc,gcd->bsgd', p, w_group)
    return y.reshape(B, S, SELF_W) * scale


def sliding_window_gqa_sinks(q, k, v, sink):
    B, S = q.shape[:2]
    nb = S // BLOCK
    qb = q.reshape(B, nb, BLOCK, N_KV_HEADS, GQA_GROUP, HEAD_DIM)
    kb = k.reshape(B, nb, BLOCK, N_KV_HEADS, HEAD_DIM)
    vb = v.reshape(B, nb, BLOCK, N_KV_HEADS, HEAD_DIM)

    def with_prev(t):
        prev = jnp.pad(t[:, :-1], ((0, 0), (1, 0), (0, 0), (0, 0), (0, 0)))
        return jnp.concatenate([prev, t], axis=2)

    kc, vc = with_prev(kb), with_prev(vb)
    s = jnp.einsum('bnqhgd,bnkhd->bnhgqk', qb, kc).astype(jnp.float32) * (HEAD_DIM ** -0.5)
    qi = jnp.arange(BLOCK)[:, None]
    kj = jnp.arange(2 * BLOCK)[None, :]
    rel = qi + BLOCK - kj
    band = (rel >= 0) & (rel < WINDOW)
    valid = (jnp.arange(nb)[:, None, None] > 0) | (kj >= BLOCK)[None]
    mask = band[None] & valid
    s = jnp.where(mask[None, :, None, None], s, NEG)
    sink_b = jnp.broadcast_to(
        sink.astype(jnp.float32).reshape(N_KV_HEADS, GQA_GROUP)[None, None, :, :, None, None],
        s.shape[:-1] + (1,))
    p = jax.nn.softmax(jnp.concatenate([s, sink_b], axis=-1), axis=-1)[..., :-1]
    o = jnp.einsum('bnhgqk,bnkhd->bnqhgd', p.astype(v.dtype), vc)
    return o.reshape(B, S, N_Q_HEADS * HEAD_DIM)


def memory_cross_attention(xq, mem_kv):
    B, S, _ = xq.shape
    M = mem_kv.shape[1]
    q = xq.reshape(B, S, XA_HEADS, XA_HEAD_DIM)
    k = mem_kv[..., :XA_W].reshape(B, M, XA_HEADS, XA_HEAD_DIM)
    v = mem_kv[..., XA_W:].reshape(B, M, XA_HEADS, XA_HEAD_DIM)
    s = jnp.einsum('bshd,bmhd->bhsm', q, k).astype(jnp.float32) * (XA_HEAD_DIM ** -0.5)
    p = jax.nn.softmax(s, axis=-1)
    return jnp.einsum('bhsm,bmhd->bshd', p.astype(v.dtype), v).reshape(B, S, XA_W)


def setup_inputs(seed: int = 0) -> dict:
    key = jax.random.key(seed)
    ks = jax.random.split(key, 20)
    f32 = jnp.float32

    def nrm(k, shape, fan_in):
        return jax.random.normal(k, shape, f32) * (fan_in ** -0.5)

    def gain(k, shape):
        return 1.0 + 0.05 * jax.random.normal(k, shape, f32)

    x = jax.random.normal(ks[0], (BATCH, SEQ, D_MODEL), f32)
    mem = jax.random.normal(ks[1], (BATCH, MEM_LEN, D_MODEL), f32)
    offset = jax.random.randint(ks[2], (BATCH, 1), 0, 4096, dtype=jnp.int32)
    positions = offset + jnp.arange(SEQ, dtype=jnp.int32)[None, :]
    return {
        "x": x,
        "mem": mem,
        "positions": positions,
        "norm_mix": gain(ks[3], (DEPTH, D_MODEL)),
        "norm_mem": gain(ks[4], (DEPTH, D_MODEL)),
        "norm_mlp": gain(ks[5], (DEPTH, D_MODEL)),
        "w_mem_kv": nrm(ks[6], (DEPTH, D_MODEL, 2 * XA_W), D_MODEL),
        "pool_w_in": nrm(ks[7], (N_POOL_LAYERS, D_MODEL, SELF_W + XA_W), D_MODEL),
        "pool_w_group": nrm(ks[8], (N_POOL_LAYERS, N_POOL_GROUPS, POOL_GROUP, POOL_GROUP), POOL_GROUP),
        "pool_scale": 1.0 + 0.1 * jax.random.normal(ks[9], (N_POOL_LAYERS, SELF_W), f32),
        "pool_w_out": nrm(ks[10], (N_POOL_LAYERS, SELF_W + XA_W, D_MODEL), SELF_W + XA_W),
        "attn_w_in": nrm(ks[11], (N_ATTN_LAYERS, D_MODEL, SELF_W + 2 * KV_W + XA_W), D_MODEL),
        "attn_sink": 0.5 * jax.random.normal(ks[12], (N_ATTN_LAYERS, N_Q_HEADS), f32),
        "attn_w_out": nrm(ks[13], (N_ATTN_LAYERS, SELF_W + XA_W, D_MODEL), SELF_W + XA_W),
        "mlp_w1": nrm(ks[14], (DEPTH, D_MODEL, D_FF), D_MODEL),
        "mlp_w2": nrm(ks[15], (DEPTH, D_FF, D_MODEL), D_FF),
        "final_norm": gain(ks[16], (D_MODEL,)),
    }


def reference(x, mem, positions, norm_mix, norm_mem, norm_mlp, w_mem_kv,
              pool_w_in, pool_w_group, pool_scale, pool_w_out,
              attn_w_in, attn_sink, attn_w_out, mlp_w1, mlp_w2, final_norm):
    B, S, _ = x.shape
    cos, sin = rope_tables(positions)
    h = x
    for i in range(DEPTH):
        j = i // N_MIXERS
        hn = rms_norm(h, norm_mix[i])
        mem_kv = rms_norm(mem, norm_mem[i]) @ w_mem_kv[i]
        if i % N_MIXERS == 0:
            proj = hn @ pool_w_in[j]
            y_self = causal_pool_mixer(proj[..., :SELF_W], pool_w_group[j], pool_scale[j])
            xq = proj[..., SELF_W:]
            w_out = pool_w_out[j]
        else:
            proj = hn @ attn_w_in[j]
            q = proj[..., :SELF_W].reshape(B, S, N_Q_HEADS, HEAD_DIM)
            k = proj[..., SELF_W:SELF_W + KV_W].reshape(B, S, N_KV_HEADS, HEAD_DIM)
            v = proj[..., SELF_W + KV_W:SELF_W + 2 * KV_W].reshape(B, S, N_KV_HEADS, HEAD_DIM)
            xq = proj[..., SELF_W + 2 * KV_W:]
            q = partial_rope(q, cos, sin)
            k = partial_rope(k, cos, sin)
            y_self = sliding_window_gqa_sinks(q, k, v, attn_sink[j])
            w_out = attn_w_out[j]
        y_mem = memory_cross_attention(xq, mem_kv)
        h = h + jnp.concatenate([y_self, y_mem], axis=-1) @ w_out
        hn = rms_norm(h, norm_mlp[i])
        h = h + jnp.square(jax.nn.relu(hn @ mlp_w1[i])) @ mlp_w2[i]
    return rms_norm(h, final_norm)
```

```python
import numpy as np
from contextlib import ExitStack
from types import SimpleNamespace
import concourse.bass as bass
import concourse.mybir as mybir
from concourse.bass_utils import run_bass_kernel_spmd

F32, BF16, I32 = mybir.dt.float32, mybir.dt.bfloat16, mybir.dt.int32
AF = mybir.ActivationFunctionType
ALU = mybir.AluOpType
NCORES = 2
EPS = 1e-6
ROPE_THETA = 500000.0


def make_cfg(D=4096, NG=4, WINDOWS=(2, 4, 8, 16), GQ=8, XH=4, NTILE=16, MEM=256, T=512, HALO=256):
    c = SimpleNamespace()
    c.D = D; c.DC = D // 128; c.SELF = 3 * D // 4; c.SC = c.SELF // 128
    c.XW = D // 4; c.XC = c.XW // 128; c.XH = XH; c.XHC = c.XC // XH; c.XD = c.XW // XH
    c.NG = NG; c.GC = c.SC // NG; c.WINDOWS = tuple(WINDOWS)
    c.NQ = c.SC; c.GQ = GQ; c.NKV = c.NQ // GQ; c.HB = min(4, GQ)
    c.FF = 4 * D; c.FCH = c.FF // 128; c.FC = 4
    c.MEM = MEM; c.MC = MEM // 128; c.T = T; c.HALO = HALO; c.NTILE = NTILE
    c.TOK = NTILE * T; c.NTOKH = HALO + c.TOK
    c.U1 = c.SC + 2 * c.NKV + c.XC
    o = 0
    c.cG = o; o += 7 * c.DC
    c.cPS = o; o += c.SC
    c.cSK = o; o += c.NQ
    c.cIF = o; o += 1
    c.cMK = o; o += 384
    c.cRM = o; o += 128
    c.NCONST = o
    def pad8(n):
        return (n + 7) // 8 * 8
    sizes = [("kv", 4 * c.XC), ("mix0", c.DC + c.NG * c.GC + c.DC), ("w1_0", c.FCH), ("w2_0", c.FCH),
             ("mix1", c.U1 + c.DC), ("w1_1", c.FCH), ("w2_1", c.FCH)]
    c.LOG = {nm: (nm, 0) for nm, n in sizes}
    c.PIECES = [(nm, pad8(n)) for nm, n in sizes]
    return c


class Op:
    __slots__ = ("eng", "fn", "deps", "sig", "chan", "cnt", "inc")


class Region:
    __slots__ = ("w", "r")

    def __init__(self):
        self.w = None
        self.r = {}


class Prog:
    ENGS = ("pe", "act", "dve", "pool", "sp")

    def __init__(self):
        self.q = {e: [] for e in self.ENGS}
        self.chans = {}

    def op(self, eng, fn, reads=(), writes=(), chan=None, inc=16):
        o = Op()
        o.eng = eng; o.fn = fn; o.sig = False; o.chan = chan; o.cnt = 0; o.inc = inc
        deps = set()
        for r in reads:
            if r.w is not None:
                deps.add(r.w)
        for w in writes:
            if w.w is not None:
                deps.add(w.w)
            deps.update(w.r.values())
        if eng == "pe":
            deps = {d for d in deps if not (d.eng == "pe" and d.chan is None)}
        o.deps = deps
        key = chan if chan is not None else eng
        for r in reads:
            r.r[key] = o
        for w in writes:
            w.w = o
            w.r = {}
        self.q[eng].append(o)
        if chan is not None:
            self.chans.setdefault(chan, []).append(o)
        return o

    def finalize(self, nc, es):
        for e in self.ENGS:
            for o in self.q[e]:
                for d in o.deps:
                    d.sig = True
        sems = {}
        totals = {}
        for e in ("pe", "act", "dve"):
            if self.q[e]:
                self.q[e][-1].sig = True
        for e in self.ENGS:
            sems[e] = es.enter_context(nc.semaphore("pg_" + e))
            n = 0
            for o in self.q[e]:
                if o.chan is None and o.sig:
                    n += 1
                    o.cnt = n
            totals[e] = n
        for ch, ops in self.chans.items():
            sems[ch] = es.enter_context(nc.semaphore("ch_" + ch))
            n = 0
            for o in ops:
                n += o.inc
                o.cnt = n
        final_waits = [(sems[ch], ops[-1].cnt) for ch, ops in self.chans.items() if ch.startswith("out") or ch.startswith("ag_")]
        final_waits += [(sems[e], totals[e]) for e in ("pe", "act", "dve") if totals[e] > 0]

        def run(eng_name, eng):
            waited = {}
            for o in self.q[eng_name]:
                need = {}
                for d in o.deps:
                    k = d.chan if d.chan is not None else d.eng
                    if d.cnt > need.get(k, 0):
                        need[k] = d.cnt
                for k in sorted(need):
                    if waited.get(k, 0) < need[k]:
                        eng.wait_ge(sems[k], need[k])
                        waited[k] = need[k]
                ins = o.fn(eng)
                if o.chan is not None:
                    ins.then_inc(sems[o.chan], o.inc)
                elif o.sig:
                    ins.then_inc(sems[eng_name], 1)
            if eng_name == "sp":
                for s, v in final_waits:
                    eng.wait_ge(s, v)

        with nc.Block() as block:
            @block.tensor
            def _(e):
                run("pe", e)

            @block.scalar
            def _(e):
                run("act", e)

            @block.vector
            def _(e):
                run("dve", e)

            @block.gpsimd
            def _(e):
                run("pool", e)

            @block.sync
            def _(e):
                run("sp", e)


def build(c):
    nc = bass.Bass("TRN2", target_bir_lowering=False)
    es = ExitStack()
    P = Prog()
    D, DC, T, SC, XC = c.D, c.DC, c.T, c.SC, c.XC

    def dram(name, shape, dt, kind):
        return nc.dram_tensor(name, list(shape), dt, kind=kind).ap()

    xT = dram("xT", [D, c.NTOKH], F32, "ExternalInput")
    memT = dram("memT", [D, c.MEM], F32, "ExternalInput")
    pos = dram("pos", [128, c.NTOKH], I32, "ExternalInput")
    consts = dram("consts", [128, c.NCONST], F32, "ExternalInput")
    invf = dram("invf", [128, c.NG * 16], F32, "ExternalInput")
    outT = dram("outT", [D, c.TOK], F32, "ExternalOutput")
    wsh, wshb, wfull = {}, {}, {}
    for name, U in c.PIECES:
        wsh[name] = dram("w_" + name, [U * 128, D], F32, "ExternalInput")

    def sb(name, shape, dt):
        return es.enter_context(nc.sbuf_tensor(name, list(shape), dt))

    h = sb("h", [128, DC, T], F32)
    A = sb("A", [128, DC, T], BF16)
    B = sb("B", [128, DC, T], BF16)
    NDED = 2
    ring = sb("ring", [128, NDED, D], BF16)
    FT = [sb("ft%d" % i, [128, 16 + T], F32) for i in range(4)]
    uhist = sb("uhist", [128, SC, 16], F32)
    kT = sb("kT", [128, c.NKV, 128 + T], BF16)
    vtm = sb("vtm", [128, 1 + T // 128, c.NKV * 128], BF16)
    ET = [sb("et%d" % i, [128, 512], BF16) for i in range(4)]
    SQ = [sb("sq%d" % i, [128, T], BF16) for i in range(2)]
    hT = sb("hT", [128, 2 * c.FC, T], BF16)
    KmT = sb("KmT", [128, 2 * XC, c.MEM], BF16)
    Vm = sb("Vm", [128, 2 * c.MC, c.XW], BF16)
    cosT = sb("cosT", [128, T], F32)
    sinT = sb("sinT", [128, T], F32)
    posi = sb("posi", [128, T], I32)
    kint = sb("kint", [128, T], I32)
    cst = sb("cst", [128, c.NCONST], F32)
    mk = sb("mk", [128, 384], BF16)
    rm = sb("rm", [128, 128], BF16)
    ones = sb("ones", [128, 128], BF16)
    esink = sb("esink", [128, c.NQ], F32)
    epst = sb("epst", [128, 1], F32)
    pst = [es.enter_context(nc.psum_tensor("ps%d" % i, [128, 512], F32)) for i in range(8)]

    Hr = [Region() for _ in range(DC)]
    Ar = [Region() for _ in range(DC)]
    Br = [Region() for _ in range(DC)]
    Sr = [Region() for _ in range(NDED)]
    FTr = [Region() for _ in range(4)]
    UHr = [Region() for _ in range(SC)]
    Kprev = [Region() for _ in range(c.NKV)]
    Kcur = [Region() for _ in range(c.NKV)]
    Vr = [Region() for _ in range(1 + T // 128)]
    ETr = [Region() for _ in range(4)]
    SQr = [Region() for _ in range(2)]
    HTr = [Region() for _ in range(2 * c.FC)]
    KmR, VmR = Region(), Region()
    CSr, TRIGr, POSr, KIr = Region(), Region(), Region(), Region()
    PSr = [Region() for _ in range(8)]
    Wpiece = {name: Region() for name, _ in c.PIECES}
    state = SimpleNamespace(ps=0, ft=0, et=0, sq=0, slot=0, nslots=NDED, och=0, nload=0)

    def next_ps():
        i = state.ps; state.ps = (i + 1) % 8
        return i

    def next_ft():
        i = state.ft; state.ft = (i + 1) % 4
        return i

    def next_et():
        i = state.et; state.et = (i + 1) % 4
        return i

    def next_sq():
        i = state.sq; state.sq = (i + 1) % 2
        return i

    def slot_ap(s):
        if s < NDED:
            return ring[:, s, :]
        k = s - NDED
        q = DC // 4
        return B[:, k * q:(k + 1) * q, :].rearrange("p a b -> p (a b)")

    def slot_regions(s):
        if s < NDED:
            return [Sr[s]]
        k = s - NDED
        q = DC // 4
        return Br[k * q:(k + 1) * q]

    def load_unit(piece, u):
        s = state.slot % state.nslots
        state.slot += 1
        ap = slot_ap(s)
        regs = slot_regions(s)
        phys, off = c.LOG[piece]
        src = wsh[phys][(off + u) * 128:(off + u + 1) * 128, :]
        state.nload += 1
        LQ = getattr(c, "LQ", 2)
        lq = "pool"
        P.op(lq, lambda e, ap=ap, src=src: e.dma_start(out=ap, in_=src),
             reads=[Wpiece[phys]], writes=regs, chan="slot%d" % s)
        return ap, regs

    def unit3(ap):
        return ap.rearrange("p (k m) -> p k m", m=128)

    def mm_group(ps_i, out_ap, pairs, reads):
        n = len(pairs)

        def fn(e):
            ins = None
            for i, (l, r) in enumerate(pairs):
                ins = e.matmul(out_ap, l, r, start=(i == 0), stop=(i == n - 1))
            return ins
        return P.op("pe", fn, reads=reads, writes=[PSr[ps_i]])

    def rmsnorm_to_A(gcol, Tt, nchunks=DC):
        ps_i = next_ps()
        for cc in range(nchunks):
            si = next_sq()
            P.op("act", lambda e, cc=cc, si=si: e.activation(SQ[si][:, :Tt], h[:, cc, :Tt], AF.Square),
                 reads=[Hr[cc]], writes=[SQr[si]])
            P.op("pe", lambda e, cc=cc, si=si: e.matmul(pst[ps_i][:, :Tt], ones[:, :], SQ[si][:, :Tt],
                                                      start=(cc == 0), stop=(cc == nchunks - 1)),
                 reads=[SQr[si], CSr], writes=[PSr[ps_i]])
        f1 = next_ft()
        P.op("act", lambda e: e.activation(FT[f1][:, :Tt], pst[ps_i][:, :Tt], AF.Sqrt,
                                           bias=epst[:, 0:1], scale=1.0 / D),
             reads=[PSr[ps_i], CSr], writes=[FTr[f1]])
        f2 = next_ft()
        P.op("dve", lambda e: e.reciprocal(FT[f2][:, :Tt], FT[f1][:, :Tt]), reads=[FTr[f1]], writes=[FTr[f2]])
        for cc in range(nchunks):
            P.op("dve", lambda e, cc=cc: e.scalar_tensor_tensor(
                out=A[:, cc, :Tt], in0=h[:, cc, :Tt], scalar=cst[:, gcol + cc:gcol + cc + 1],
                in1=FT[f2][:, :Tt], op0=ALU.mult, op1=ALU.mult),
                reads=[Hr[cc], FTr[f2], CSr], writes=[Ar[cc]])
        return f2

    def proj_unit(piece, u, Tt, kc_n=DC, src=A, src_r=Ar):
        ap, regs = load_unit(piece, u)
        u3 = unit3(ap)
        ps_i = next_ps()
        mm_group(ps_i, pst[ps_i][:, :Tt], [(u3[:, kc, :], src[:, kc, :Tt]) for kc in range(kc_n)],
                 reads=regs + src_r[:kc_n])
        return ps_i

    def out_proj(piece, u0, Tt):
        for dc in range(DC):
            ps_i = proj_unit(piece, u0 + dc, Tt)
            P.op("dve", lambda e, dc=dc, ps_i=ps_i: e.tensor_tensor(out=h[:, dc, :Tt], in0=h[:, dc, :Tt],
                                                                    in1=pst[ps_i][:, :Tt], op=ALU.add),
                 reads=[PSr[ps_i], Hr[dc]], writes=[Hr[dc]])

    def mlp(layer, Tt):
        rmsnorm_to_A(c.cG + (4 + layer) * DC, Tt)
        state.nslots = NDED + 4
        state.slot = 0
        w1, w2 = "w1_%d" % layer, "w2_%d" % layer
        NS = c.FCH // c.FC

        def phaseA(s):
            par = s % 2
            for f in range(c.FC):
                ps_i = proj_unit(w1, s * c.FC + f, Tt)
                fi = next_ft()
                P.op("act", lambda e, ps_i=ps_i, fi=fi: e.activation(FT[fi][:, :Tt], pst[ps_i][:, :Tt], AF.Relu),
                     reads=[PSr[ps_i]], writes=[FTr[fi]])
                hi = par * c.FC + f
                P.op("act", lambda e, fi=fi, hi=hi: e.activation(hT[:, hi, :Tt], FT[fi][:, :Tt], AF.Square),
                     reads=[FTr[fi]], writes=[HTr[hi]])

        def phaseB(s):
            par = s % 2
            units = [load_unit(w2, s * c.FC + f) for f in range(c.FC)]
            regs = [r for _, rs in units for r in rs]
            for dc in range(DC):
                ps_i = next_ps()
                mm_group(ps_i, pst[ps_i][:, :Tt],
                         [(units[f][0][:, dc * 128:(dc + 1) * 128], hT[:, par * c.FC + f, :Tt]) for f in range(c.FC)],
                         reads=regs + HTr[par * c.FC:(par + 1) * c.FC])
                P.op("dve", lambda e, dc=dc, ps_i=ps_i: e.tensor_tensor(out=h[:, dc, :Tt], in0=h[:, dc, :Tt],
                                                                        in1=pst[ps_i][:, :Tt], op=ALU.add),
                     reads=[PSr[ps_i], Hr[dc]], writes=[Hr[dc]])

        phaseA(0)
        for s in range(NS):
            if s + 1 < NS:
                phaseA(s + 1)
            phaseB(s)
        state.nslots = NDED
        state.slot = 0

    def cross_attn(layer, Tt):
        sc = float(c.XD) ** -0.5
        for hx in range(c.XH):
            ets = []
            for mc in range(c.MC):
                ps_i = next_ps()
                mm_group(ps_i, pst[ps_i][:, :Tt],
                         [(KmT[:, layer * XC + hx * c.XHC + dc, mc * 128:(mc + 1) * 128], B[:, SC + hx * c.XHC + dc, :Tt])
                          for dc in range(c.XHC)],
                         reads=[KmR] + Br[SC + hx * c.XHC: SC + (hx + 1) * c.XHC])
                ei = next_et()
                P.op("act", lambda e, ps_i=ps_i, ei=ei: e.activation(ET[ei][:, :Tt], pst[ps_i][:, :Tt], AF.Exp, scale=sc),
                     reads=[PSr[ps_i]], writes=[ETr[ei]])
                ets.append(ei)
            pd = next_ps()
            mm_group(pd, pst[pd][:, :Tt], [(ones[:, :], ET[ei][:, :Tt]) for ei in ets],
                     reads=[ETr[ei] for ei in ets] + [CSr])
            fi = next_ft()
            P.op("dve", lambda e, pd=pd, fi=fi: e.reciprocal(FT[fi][:, :Tt], pst[pd][:, :Tt]),
                 reads=[PSr[pd]], writes=[FTr[fi]])
            for dc in range(c.XHC):
                po = next_ps()
                col = (hx * c.XHC + dc) * 128
                mm_group(po, pst[po][:, :Tt],
                         [(Vm[:, layer * c.MC + mc, col:col + 128], ET[ets[mc]][:, :Tt]) for mc in range(c.MC)],
                         reads=[VmR] + [ETr[ei] for ei in ets])
                yc = SC + hx * c.XHC + dc
                P.op("dve", lambda e, po=po, fi=fi, yc=yc: e.tensor_tensor(out=A[:, yc, :Tt], in0=pst[po][:, :Tt],
                                                                           in1=FT[fi][:, :Tt], op=ALU.mult),
                     reads=[PSr[po], FTr[fi]], writes=[Ar[yc]])

    def mixer0(Tt, first_real):
        rmsnorm_to_A(c.cG + 0 * DC, Tt)
        for j in range(DC):
            ps_i = proj_unit("mix0", j, Tt)
            if j >= SC:
                P.op("act", lambda e, j=j, ps_i=ps_i: e.activation(B[:, j, :Tt], pst[ps_i][:, :Tt], AF.Copy),
                     reads=[PSr[ps_i]], writes=[Br[j]])
                continue
            g = j // c.GC
            w = c.WINDOWS[g]
            ui = next_ft()
            ub = FT[ui]
            P.op("dve", lambda e, ub=ub, j=j: e.tensor_copy(ub[:, 0:16], uhist[:, j, :]),
                 reads=[UHr[j]], writes=[FTr[ui]])
            P.op("act", lambda e, ub=ub, ps_i=ps_i: e.activation(ub[:, 16:16 + Tt], pst[ps_i][:, :Tt], AF.Copy),
                 reads=[PSr[ps_i], FTr[ui]], writes=[FTr[ui]])
            P.op("dve", lambda e, ub=ub, j=j: e.tensor_copy(uhist[:, j, :], ub[:, Tt:Tt + 16]),
                 reads=[FTr[ui]], writes=[UHr[j]])
            cur, curi = ub, ui
            step = 1
            W = 16 + Tt
            scratch = [next_ft(), next_ft()]
            nstep = 0
            while step < w:
                ni = scratch[nstep % 2]
                nstep += 1
                nx = FT[ni]
                lo = 2 * step - 1
                P.op("dve", lambda e, nx=nx, cur=cur, lo=lo, step=step: e.tensor_tensor(
                    out=nx[:, lo:W], in0=cur[:, lo:W], in1=cur[:, lo - step:W - step], op=ALU.add),
                    reads=[FTr[curi]], writes=[FTr[ni]])
                cur, curi = nx, ni
                step *= 2
            P.op("dve", lambda e, cur=cur, ub=ub, j=j, w=w: e.scalar_tensor_tensor(
                out=B[:, j, :Tt], in0=cur[:, 16:16 + Tt], scalar=1.0 / w, in1=ub[:, 16:16 + Tt],
                op0=ALU.mult, op1=ALU.subtract),
                reads=[FTr[curi], FTr[ui]], writes=[Br[j]])
            if first_real:
                ti = next_ft()
                P.op("dve", lambda e, ti=ti, cur=cur, g=g: e.tensor_tensor(
                    out=FT[ti][:, :16], in0=cur[:, 16:32], in1=cst_inv[:, g * 16:(g + 1) * 16], op=ALU.mult),
                    reads=[FTr[curi], CIr], writes=[FTr[ti]])
                P.op("dve", lambda e, ti=ti, ub=ub, j=j: e.tensor_tensor(
                    out=B[:, j, :16], in0=FT[ti][:, :16], in1=ub[:, 16:32], op=ALU.subtract),
                    reads=[FTr[ti], FTr[ui]], writes=[Br[j]])
        SUB = getattr(c, "SUB", 99)
        if SUB <= 1:
            return
        for g in range(c.NG):
            for oc in range(c.GC):
                ap, regs = load_unit("mix0", DC + g * c.GC + oc)
                u3 = unit3(ap)
                ps_i = next_ps()
                mm_group(ps_i, pst[ps_i][:, :Tt],
                         [(u3[:, ic, :], B[:, g * c.GC + ic, :Tt]) for ic in range(c.GC)],
                         reads=regs + Br[g * c.GC:(g + 1) * c.GC])
                yc = g * c.GC + oc
                P.op("act", lambda e, yc=yc, ps_i=ps_i: e.activation(A[:, yc, :Tt], pst[ps_i][:, :Tt], AF.Identity,
                                                                     scale=cst[:, c.cPS + yc:c.cPS + yc + 1]),
                     reads=[PSr[ps_i], CSr], writes=[Ar[yc]])
        if SUB <= 2:
            return
        cross_attn(0, Tt)
        if SUB <= 3:
            return
        out_proj("mix0", DC + c.NG * c.GC, Tt)

    def rope_tables(t0, Tt):
        P.op("pool", lambda e: e.dma_start(out=posi[:, :Tt], in_=pos[:, t0:t0 + Tt]), reads=[], writes=[POSr], chan="pos")
        fa = next_ft()
        P.op("dve", lambda e: e.tensor_copy(FT[fa][:, :Tt], posi[:, :Tt]), reads=[POSr], writes=[FTr[fa]])
        P.op("dve", lambda e: e.tensor_scalar(FT[fa][:, :Tt], FT[fa][:, :Tt], cst[:, c.cIF:c.cIF + 1], None,
                                              op0=ALU.mult),
             reads=[FTr[fa], CSr], writes=[FTr[fa]])
        two_pi = 2.0 * np.pi

        def reduced_sin(shift, dst):
            MAGIC = 12582912.0
            fb = next_ft()
            P.op("dve", lambda e: e.tensor_scalar(FT[fb][:, :Tt], FT[fa][:, :Tt], shift, 1.0 / two_pi,
                                                  op0=ALU.add, op1=ALU.mult),
                 reads=[FTr[fa]], writes=[FTr[fb]])
            P.op("dve", lambda e: e.tensor_scalar(FT[fb][:, :Tt], FT[fb][:, :Tt], MAGIC, None, op0=ALU.add),
                 reads=[FTr[fb]], writes=[FTr[fb]])
            P.op("dve", lambda e: e.tensor_scalar(FT[fb][:, :Tt], FT[fb][:, :Tt], -MAGIC, None, op0=ALU.add),
                 reads=[FTr[fb]], writes=[FTr[fb]])
            fc = next_ft()
            P.op("dve", lambda e: e.tensor_scalar(FT[fc][:, :Tt], FT[fa][:, :Tt], shift, None, op0=ALU.add),
                 reads=[FTr[fa]], writes=[FTr[fc]])
            P.op("dve", lambda e: e.scalar_tensor_tensor(out=FT[fc][:, :Tt], in0=FT[fb][:, :Tt], scalar=-two_pi,
                                                         in1=FT[fc][:, :Tt], op0=ALU.mult, op1=ALU.add),
                 reads=[FTr[fb], FTr[fc]], writes=[FTr[fc]])
            P.op("dve", lambda e: e.tensor_scalar(FT[fc][:, :Tt], FT[fc][:, :Tt], -3.1415925, 3.1415925,
                                                  op0=ALU.max, op1=ALU.min),
                 reads=[FTr[fc]], writes=[FTr[fc]])
            P.op("act", lambda e: e.activation(dst[:, :Tt], FT[fc][:, :Tt], AF.Sin), reads=[FTr[fc]], writes=[TRIGr])

        reduced_sin(0.0, sinT)
        reduced_sin(0.5 * np.pi, cosT)

    def rope_evac(ps_i, dst_ap, dst_regs, Tt):
        P.op("act", lambda e: e.activation(dst_ap, pst[ps_i][:, :Tt], AF.Copy), reads=[PSr[ps_i]], writes=dst_regs)
        if getattr(c, "NOROT", 0) == 1:
            return
        pr = next_ps()
        rv = getattr(c, "ROTV", 0)
        l_ap = ones[:, :] if rv == 2 else rm[:, :]
        r_ap = A[:, 0, :Tt] if rv in (1, 3) else dst_ap
        P.op("pe", lambda e: e.matmul(pst[pr][:, :Tt], l_ap, r_ap, start=True, stop=True),
             reads=([CSr] if rv == 3 else dst_regs + [CSr]), writes=[PSr[pr]])
        if getattr(c, "NOROT", 0) == 2:
            return
        f1 = next_ft()
        P.op("dve", lambda e: e.tensor_tensor(out=FT[f1][:, :Tt], in0=pst[ps_i][:, :Tt], in1=cosT[:, :Tt], op=ALU.mult),
             reads=[PSr[ps_i], TRIGr] + list(dst_regs), writes=[FTr[f1]])
        if getattr(c, "NOROT", 0) == 3:
            return
        f2 = next_ft()
        P.op("dve", lambda e: e.tensor_tensor(out=FT[f2][:, :Tt], in0=pst[pr][:, :Tt], in1=sinT[:, :Tt], op=ALU.mult),
             reads=[PSr[pr], TRIGr], writes=[FTr[f2]])
        P.op("dve", lambda e: e.tensor_tensor(out=dst_ap, in0=FT[f1][:, :Tt], in1=FT[f2][:, :Tt], op=ALU.add),
             reads=[FTr[f1], FTr[f2]], writes=dst_regs)

    def mixer1(Tt, t0, halo, first_real):
        rmsnorm_to_A(c.cG + 1 * DC, Tt)
        rope_tables(t0, Tt)
        nb = Tt // 128
        SUB1 = getattr(c, "SUB1", 99)
        if SUB1 <= 1:
            return
        if not halo:
            for j in range(SC):
                ps_i = proj_unit("mix1", j, Tt)
                rope_evac(ps_i, B[:, j, :Tt], [Br[j]], Tt)
        for g in range(c.NKV):
            ps_i = proj_unit("mix1", SC + g, Tt)
            rope_evac(ps_i, kT[:, g, 128:128 + Tt], [Kcur[g]], Tt)
        if SUB1 <= 2:
            return
        for g in range(c.NKV):
            ap, regs = load_unit("mix1", SC + c.NKV + g)
            u3 = unit3(ap)
            ps_i = next_ps()
            for blk in range(nb):
                mm_group(ps_i, pst[ps_i][:, blk * 128:(blk + 1) * 128],
                         [(A[:, kc, blk * 128:(blk + 1) * 128], u3[:, kc, :]) for kc in range(DC)],
                         reads=regs + Ar)
            P.op("act", lambda e, g=g, ps_i=ps_i: e.activation(
                vtm[:, 1:1 + nb, g * 128:(g + 1) * 128],
                pst[ps_i][:, :nb * 128].rearrange("p (b m) -> p b m", m=128), AF.Copy),
                reads=[PSr[ps_i]], writes=Vr[1:1 + nb])
        if not halo:
            for i in range(XC):
                ps_i = proj_unit("mix1", SC + 2 * c.NKV + i, Tt)
                P.op("act", lambda e, i=i, ps_i=ps_i: e.activation(B[:, SC + i, :Tt], pst[ps_i][:, :Tt], AF.Copy),
                     reads=[PSr[ps_i]], writes=[Br[SC + i]])
            sc = 128.0 ** -0.5
            HB = c.HB
            NW = HB * 128
            for g in range(c.NKV):
                for blk in range(nb):
                    for half in range(c.GQ // HB):
                        h0 = g * c.GQ + half * HB
                        qap = B[:, h0:h0 + HB, blk * 128:(blk + 1) * 128]
                        qregs = Br[h0:h0 + HB]
                        ets = []
                        for which in range(2):
                            kap = kT[:, g, blk * 128 + which * 128: blk * 128 + which * 128 + 128]
                            kreg = [Kprev[g], Kcur[g]] if blk == 0 and which == 0 else [Kcur[g]]
                            ps_i = next_ps()
                            P.op("pe", lambda e, ps_i=ps_i, kap=kap, qap=qap: e.matmul(
                                pst[ps_i][:, :NW].rearrange("p (a b) -> p a b", b=128), kap, qap, start=True, stop=True),
                                reads=kreg + qregs, writes=[PSr[ps_i]])
                            ei = next_et()
                            P.op("act", lambda e, ps_i=ps_i, ei=ei: e.activation(ET[ei][:, :NW], pst[ps_i][:, :NW], AF.Exp, scale=sc),
                                 reads=[PSr[ps_i]], writes=[ETr[ei]])
                            if which == 1:
                                mcol = 0
                            else:
                                mcol = 256 if (first_real and blk == 0) else 128
                            mt = mk[:, mcol:mcol + 128]
                            mb = bass.AP(mt.tensor, mt.offset, [mt.ap[0], [0, HB], mt.ap[1]])
                            P.op("dve", lambda e, ei=ei, mb=mb: e.tensor_tensor(
                                out=ET[ei][:, :NW].rearrange("p (a b) -> p a b", b=128),
                                in0=ET[ei][:, :NW].rearrange("p (a b) -> p a b", b=128), in1=mb, op=ALU.mult),
                                reads=[ETr[ei], CSr], writes=[ETr[ei]])
                            ets.append(ei)
                        pd = next_ps()
                        mm_group(pd, pst[pd][:, :NW], [(ones[:, :], ET[ei][:, :NW]) for ei in ets],
                                 reads=[ETr[ei] for ei in ets] + [CSr])
                        po = next_ps()
                        mm_group(po, pst[po][:, :NW],
                                 [(vtm[:, blk + which, g * 128:(g + 1) * 128], ET[ets[which]][:, :NW]) for which in range(2)],
                                 reads=[ETr[ei] for ei in ets] + [Vr[blk], Vr[blk + 1]])
                        fi = next_ft()
                        st = esink[:, h0:h0 + HB]
                        sbc = bass.AP(st.tensor, st.offset, [st.ap[0], st.ap[1], [0, 128]])
                        P.op("dve", lambda e, pd=pd, fi=fi, sbc=sbc: e.tensor_tensor(
                            out=FT[fi][:, :NW].rearrange("p (a b) -> p a b", b=128),
                            in0=pst[pd][:, :NW].rearrange("p (a b) -> p a b", b=128), in1=sbc, op=ALU.add),
                            reads=[PSr[pd], CSr], writes=[FTr[fi]])
                        P.op("dve", lambda e, fi=fi: e.reciprocal(FT[fi][:, :NW], FT[fi][:, :NW]),
                             reads=[FTr[fi]], writes=[FTr[fi]])
                        P.op("dve", lambda e, po=po, fi=fi, h0=h0, blk=blk: e.tensor_tensor(
                            out=A[:, h0:h0 + HB, blk * 128:(blk + 1) * 128],
                            in0=pst[po][:, :NW].rearrange("p (a b) -> p a b", b=128),
                            in1=FT[fi][:, :NW].rearrange("p (a b) -> p a b", b=128), op=ALU.mult),
                            reads=[PSr[po], FTr[fi]], writes=Ar[h0:h0 + HB])
        if SUB1 <= 3:
            return
        for g in range(c.NKV):
            P.op("dve", lambda e, g=g: e.tensor_copy(kT[:, g, 0:128], kT[:, g, Tt:Tt + 128]),
                 reads=[Kcur[g]], writes=[Kprev[g]])
        P.op("dve", lambda e: e.tensor_copy(vtm[:, 0, :], vtm[:, nb, :]), reads=[Vr[nb]], writes=[Vr[0]])
        if not halo:
            cross_attn(1, Tt)
            out_proj("mix1", c.U1, Tt)

    cst_inv = sb("cst_inv", [128, c.NG * 16], F32)
    P.op("sp", lambda e: e.dma_start(out=cst[:, :], in_=consts), writes=[CSr], chan="cst")
    CIr = Region()
    P.op("sp", lambda e: e.dma_start(out=cst_inv[:, :], in_=invf), writes=[CIr], chan="cst2")
    P.op("dve", lambda e: e.tensor_copy(mk[:, :], cst[:, c.cMK:c.cMK + 384]), reads=[CSr], writes=[CSr])
    P.op("dve", lambda e: e.tensor_copy(rm[:, :], cst[:, c.cRM:c.cRM + 128]), reads=[CSr], writes=[CSr])
    P.op("dve", lambda e: e.memset(ones[:, :], 1.0), writes=[CSr])
    P.op("dve", lambda e: e.memset(epst[:, :], EPS), writes=[CSr])
    P.op("dve", lambda e: e.memset(uhist[:, :, :], 0.0), writes=UHr)
    P.op("dve", lambda e: e.memset(kT[:, :, :], 0.0), writes=Kprev + Kcur)
    P.op("dve", lambda e: e.memset(vtm[:, :, :], 0.0), writes=Vr)
    P.op("act", lambda e: e.activation(esink[:, :], cst[:, c.cSK:c.cSK + c.NQ], AF.Exp), reads=[CSr], writes=[CSr])
    STAGE = getattr(c, "STAGE", 99)
    for q4 in range(4 if STAGE >= 2 else 0):
        q = DC // 4
        P.op("sp", lambda e, q4=q4, q=q: e.dma_start(
            out=h[:, q4 * q:(q4 + 1) * q, :c.MEM],
            in_=memT[q4 * q * 128:(q4 + 1) * q * 128, :].rearrange("(c p) t -> p c t", p=128)),
            writes=Hr[q4 * q:(q4 + 1) * q], chan="x%d" % q4)
    for layer in range(2 if STAGE >= 2 else 0):
        rmsnorm_to_A(c.cG + (2 + layer) * DC, c.MEM)
        for j in range(XC):
            ps_i = proj_unit("kv", layer * 2 * XC + j, c.MEM)
            P.op("act", lambda e, j=j, ps_i=ps_i, layer=layer: e.activation(KmT[:, layer * XC + j, :], pst[ps_i][:, :c.MEM], AF.Copy),
                 reads=[PSr[ps_i]], writes=[KmR])
        for j in range(XC):
            ap, regs = load_unit("kv", layer * 2 * XC + XC + j)
            u3 = unit3(ap)
            ps_i = next_ps()
            for mc in range(c.MC):
                mm_group(ps_i, pst[ps_i][:, mc * 128:(mc + 1) * 128],
                         [(A[:, kc, mc * 128:(mc + 1) * 128], u3[:, kc, :]) for kc in range(DC)],
                         reads=regs + Ar)
            P.op("act", lambda e, j=j, ps_i=ps_i, layer=layer: e.activation(
                Vm[:, layer * c.MC:(layer + 1) * c.MC, j * 128:(j + 1) * 128],
                pst[ps_i][:, :c.MC * 128].rearrange("p (b m) -> p b m", m=128), AF.Copy),
                reads=[PSr[ps_i]], writes=[VmR])

    tiles = [(0, c.HALO, True)] + [(c.HALO + i * T, T, False) for i in range(c.NTILE)]
    for ti, (t0, Tt, halo) in enumerate(tiles if STAGE >= 3 else []):
        first_real = (ti == 1)
        for q4 in range(4):
            q = DC // 4
            P.op("pool", lambda e, q4=q4, q=q, t0=t0, Tt=Tt: e.dma_start(
                out=h[:, q4 * q:(q4 + 1) * q, :Tt],
                in_=xT[q4 * q * 128:(q4 + 1) * q * 128, t0:t0 + Tt].rearrange("(c p) t -> p c t", p=128)),
                writes=Hr[q4 * q:(q4 + 1) * q], chan="x%d" % q4)
        DBG = getattr(c, "DBG", -1)

        def dump_h():
            o0 = t0 - c.HALO
            for q4 in range(4):
                q = DC // 4
                P.op("pool", lambda e, q4=q4, q=q, o0=o0, Tt=Tt: e.dma_start(
                    out=outT[q4 * q * 128:(q4 + 1) * q * 128, o0:o0 + Tt].rearrange("(c p) t -> p c t", p=128),
                    in_=h[:, q4 * q:(q4 + 1) * q, :Tt]),
                    reads=Hr[q4 * q:(q4 + 1) * q], chan="out%d" % q4)
        if DBG == 0 and not halo:
            dump_h(); continue
        mixer0(Tt, first_real)
        if DBG == 1 and not halo:
            dump_h(); continue
        if STAGE == 3:
            break
        mlp(0, Tt)
        if STAGE == 4:
            if getattr(c, "ROTV", 0) == 9:
                mlp(0, Tt)
            break
        if DBG == 2 and not halo:
            dump_h(); continue
        mixer1(Tt, t0, halo, first_real)
        if DBG == 3 and not halo:
            dump_h(); continue
        if STAGE == 5:
            break
        if halo:
            continue
        mlp(1, Tt)
        ps_i = next_ps()
        for cc in range(DC):
            si = next_sq()
            P.op("act", lambda e, cc=cc, si=si: e.activation(SQ[si][:, :Tt], h[:, cc, :Tt], AF.Square),
                 reads=[Hr[cc]], writes=[SQr[si]])
            P.op("pe", lambda e, cc=cc, si=si, ps_i=ps_i: e.matmul(pst[ps_i][:, :Tt], ones[:, :], SQ[si][:, :Tt],
                                                                  start=(cc == 0), stop=(cc == DC - 1)),
                 reads=[SQr[si], CSr], writes=[PSr[ps_i]])
        f1 = next_ft()
        P.op("act", lambda e, f1=f1, ps_i=ps_i: e.activation(FT[f1][:, :Tt], pst[ps_i][:, :Tt], AF.Sqrt,
                                                             bias=epst[:, 0:1], scale=1.0 / D),
             reads=[PSr[ps_i], CSr], writes=[FTr[f1]])
        f2 = next_ft()
        P.op("dve", lambda e, f1=f1, f2=f2: e.reciprocal(FT[f2][:, :Tt], FT[f1][:, :Tt]), reads=[FTr[f1]], writes=[FTr[f2]])
        gcol = c.cG + 6 * DC
        for cc in range(DC):
            P.op("dve", lambda e, cc=cc, f2=f2: e.scalar_tensor_tensor(
                out=h[:, cc, :Tt], in0=h[:, cc, :Tt], scalar=cst[:, gcol + cc:gcol + cc + 1],
                in1=FT[f2][:, :Tt], op0=ALU.mult, op1=ALU.mult),
                reads=[Hr[cc], FTr[f2], CSr], writes=[Hr[cc]])
        o0 = t0 - c.HALO
        for q4 in range(4):
            q = DC // 4
            P.op("pool", lambda e, q4=q4, q=q, o0=o0, Tt=Tt: e.dma_start(
                out=outT[q4 * q * 128:(q4 + 1) * q * 128, o0:o0 + Tt].rearrange("(c p) t -> p c t", p=128),
                in_=h[:, q4 * q:(q4 + 1) * q, :Tt]),
                reads=Hr[q4 * q:(q4 + 1) * q], chan="out%d" % q4)

    P.finalize(nc, es)
    es.close()
    return nc


def _units_in(W):
    K, N = W.shape
    KC, U = K // 128, N // 128
    return W.reshape(KC, 128, U, 128).transpose(2, 1, 0, 3).reshape(U, 128, KC * 128)


def _pad_units(u, n, width):
    U, p, w = u.shape
    out = np.zeros((n, 128, width), np.float32)
    out[:U, :, :w] = u
    return out


def _struct_consts(c):
    inv_freq = (ROPE_THETA ** (-np.arange(0, 32, 2, dtype=np.float32) / np.float32(32))).astype(np.float32)
    col = np.zeros((128,), np.float32)
    col[:32] = np.concatenate([inv_freq, inv_freq])
    k = np.arange(128)[:, None]
    q = np.arange(128)[None, :]
    m_cur = (k <= q).astype(np.float32)
    m_prev = (k > q).astype(np.float32)
    rmat = np.zeros((128, 128), np.float32)
    for m in range(16):
        rmat[m + 16, m] = -1.0
        rmat[m, m + 16] = 1.0
    return col, m_cur, m_prev, rmat


def prep_inputs(c, inp):
    f = lambda a: np.asarray(a, dtype=np.float32)
    x, mem, positions = f(inp["x"]), f(inp["mem"]), np.asarray(inp["positions"]).astype(np.int32)
    D, DC = c.D, c.DC
    col, m_cur, m_prev, rmat = _struct_consts(c)
    wkv = f(inp["w_mem_kv"])
    kv_units = np.concatenate([_units_in(wkv[0]), _units_in(wkv[1])], 0)
    wg = f(inp["pool_w_group"])[0]
    wg_units = np.concatenate([_pad_units(_units_in(wg[g]), c.GC, D) for g in range(c.NG)], 0)
    mix0 = np.concatenate([_units_in(f(inp["pool_w_in"])[0]), wg_units, _units_in(f(inp["pool_w_out"])[0])], 0)
    mix1 = np.concatenate([_units_in(f(inp["attn_w_in"])[0]), _units_in(f(inp["attn_w_out"])[0])], 0)
    w1 = f(inp["mlp_w1"]); w2 = f(inp["mlp_w2"])
    pieces = {
        "kv": kv_units, "mix0": mix0, "mix1": mix1,
        "w1_0": _units_in(w1[0]), "w2_0": w2[0].reshape(c.FCH, 128, D),
        "w1_1": _units_in(w1[1]), "w2_1": w2[1].reshape(c.FCH, 128, D),
    }
    shards = {}
    for name, U in c.PIECES:
        u = pieces[name]
        if u.shape[0] != U:
            u = _pad_units(u, U, D)
        u = np.ascontiguousarray(u).reshape(U * 128, D)
        shards[name] = [u for i in range(NCORES)]
    def percol(v):
        return np.ascontiguousarray(v.reshape(-1, 128).T)
    gains = [percol(f(inp["norm_mix"])[0]), percol(f(inp["norm_mix"])[1]),
             percol(f(inp["norm_mem"])[0]), percol(f(inp["norm_mem"])[1]),
             percol(f(inp["norm_mlp"])[0]), percol(f(inp["norm_mlp"])[1]),
             percol(f(inp["final_norm"]))]
    ps = percol(f(inp["pool_scale"])[0])
    sink = np.broadcast_to(f(inp["attn_sink"])[0][None, :], (128, c.NQ))
    in_maps = []
    nq = NCORES // 2
    for core in range(NCORES):
        b, qd = core // nq, core % nq
        s0 = qd * c.TOK
        xs = np.zeros((c.NTOKH, D), np.float32)
        ps_ = np.zeros((c.NTOKH,), np.int32)
        if qd == 0:
            xs[c.HALO:] = x[b, s0:s0 + c.TOK]
            ps_[c.HALO:] = positions[b, s0:s0 + c.TOK]
            m_pf = np.zeros_like(m_prev)
        else:
            xs[:] = x[b, s0 - c.HALO:s0 + c.TOK]
            ps_[:] = positions[b, s0 - c.HALO:s0 + c.TOK]
            m_pf = m_prev
        consts = np.concatenate(gains + [ps, sink, col[:, None], m_cur, m_prev, m_pf, rmat], axis=1).astype(np.float32)
        assert consts.shape == (128, c.NCONST), consts.shape
        inv = np.zeros((c.NG, 16), np.float32)
        for g, w in enumerate(c.WINDOWS):
            inv[g, :] = np.float32(1.0) / np.float32(w)
            if qd == 0:
                n = np.minimum(np.arange(1, 17), w).astype(np.float32)
                inv[g, :] = np.float32(1.0) / n
        m = {
            "xT": np.ascontiguousarray(xs.T),
            "memT": np.ascontiguousarray(mem[b].T),
            "pos": np.ascontiguousarray(np.broadcast_to(ps_[None, :], (128, c.NTOKH))),
            "consts": np.ascontiguousarray(consts),
            "invf": np.ascontiguousarray(np.broadcast_to(inv.reshape(1, -1), (128, c.NG * 16))),
        }
        for name, _ in c.PIECES:
            m["w_" + name] = np.ascontiguousarray(shards[name][core])
        in_maps.append(m)
    return in_maps


def run(c, inp):
    nc = build(c)
    in_maps = prep_inputs(c, inp)
    res = run_bass_kernel_spmd(nc, in_maps, core_ids=list(range(NCORES)))
    nq = NCORES // 2
    B = np.asarray(inp["x"]).shape[0]
    out = np.zeros((B, nq * c.TOK, c.D), np.float32)
    for core in range(NCORES):
        b, qd = core // nq, core % nq
        out[b, qd * c.TOK:(qd + 1) * c.TOK, :] = np.asarray(res.results[core]["outT"]).T
    return out


def kernel(**inputs):
    return run(make_cfg(), inputs)
```

```python
import numpy as np
from contextlib import ExitStack
from types import SimpleNamespace
import concourse.bass as bass
import concourse.mybir as mybir
from concourse.bass_utils import run_bass_kernel_spmd

F32, BF16, I32 = mybir.dt.float32, mybir.dt.bfloat16, mybir.dt.int32
AF = mybir.ActivationFunctionType
ALU = mybir.AluOpType
NCORES = 8
EPS = 1e-6
ROPE_THETA = 500000.0


def make_cfg(D=4096, NG=4, WINDOWS=(2, 4, 8, 16), GQ=8, XH=4, NTILE=4, MEM=256, T=512, HALO=256):
    c = SimpleNamespace()
    c.D = D; c.DC = D // 128; c.SELF = 3 * D // 4; c.SC = c.SELF // 128
    c.XW = D // 4; c.XC = c.XW // 128; c.XH = XH; c.XHC = c.XC // XH; c.XD = c.XW // XH
    c.NG = NG; c.GC = c.SC // NG; c.WINDOWS = tuple(WINDOWS)
    c.NQ = c.SC; c.GQ = GQ; c.NKV = c.NQ // GQ; c.HB = min(4, GQ)
    c.FF = 4 * D; c.FCH = c.FF // 128; c.FC = 4
    c.MEM = MEM; c.MC = MEM // 128; c.T = T; c.HALO = HALO; c.NTILE = NTILE
    c.TOK = NTILE * T; c.NTOKH = HALO + c.TOK
    c.U1 = c.SC + 2 * c.NKV + c.XC
    o = 0
    c.cG = o; o += 7 * c.DC
    c.cPS = o; o += c.SC
    c.cSK = o; o += c.NQ
    c.cIF = o; o += 1
    c.cMK = o; o += 384
    c.cRM = o; o += 128
    c.NCONST = o
    def pad8(n):
        return (n + 7) // 8 * 8
    sizes = [("kv", 4 * c.XC), ("mix0", c.DC + c.NG * c.GC + c.DC), ("w1_0", c.FCH), ("w2_0", c.FCH),
             ("mix1", c.U1 + c.DC), ("w1_1", c.FCH), ("w2_1", c.FCH)]
    c.LOG = {nm: (nm, 0) for nm, n in sizes}
    c.PIECES = [(nm, pad8(n)) for nm, n in sizes]
    return c


class Op:
    __slots__ = ("eng", "fn", "deps", "sig", "chan", "cnt", "inc")


class Region:
    __slots__ = ("w", "r")

    def __init__(self):
        self.w = None
        self.r = {}


class Prog:
    ENGS = ("pe", "act", "dve", "pool", "sp")

    def __init__(self):
        self.q = {e: [] for e in self.ENGS}
        self.chans = {}

    def op(self, eng, fn, reads=(), writes=(), chan=None, inc=16):
        o = Op()
        o.eng = eng; o.fn = fn; o.sig = False; o.chan = chan; o.cnt = 0; o.inc = inc
        deps = set()
        for r in reads:
            if r.w is not None:
                deps.add(r.w)
        for w in writes:
            if w.w is not None:
                deps.add(w.w)
            deps.update(w.r.values())
        if eng == "pe":
            deps = {d for d in deps if not (d.eng == "pe" and d.chan is None)}
        o.deps = deps
        key = chan if chan is not None else eng
        for r in reads:
            r.r[key] = o
        for w in writes:
            w.w = o
            w.r = {}
        self.q[eng].append(o)
        if chan is not None:
            self.chans.setdefault(chan, []).append(o)
        return o

    def finalize(self, nc, es):
        for e in self.ENGS:
            for o in self.q[e]:
                for d in o.deps:
                    d.sig = True
        sems = {}
        totals = {}
        for e in ("pe", "act", "dve"):
            if self.q[e]:
                self.q[e][-1].sig = True
        for e in self.ENGS:
            sems[e] = es.enter_context(nc.semaphore("pg_" + e))
            n = 0
            for o in self.q[e]:
                if o.chan is None and o.sig:
                    n += 1
                    o.cnt = n
            totals[e] = n
        for ch, ops in self.chans.items():
            sems[ch] = es.enter_context(nc.semaphore("ch_" + ch))
            n = 0
            for o in ops:
                n += o.inc
                o.cnt = n
        final_waits = [(sems[ch], ops[-1].cnt) for ch, ops in self.chans.items() if ch.startswith("out") or ch.startswith("ag_")]
        final_waits += [(sems[e], totals[e]) for e in ("pe", "act", "dve") if totals[e] > 0]

        def run(eng_name, eng):
            waited = {}
            for o in self.q[eng_name]:
                need = {}
                for d in o.deps:
                    k = d.chan if d.chan is not None else d.eng
                    if d.cnt > need.get(k, 0):
                        need[k] = d.cnt
                for k in sorted(need):
                    if waited.get(k, 0) < need[k]:
                        eng.wait_ge(sems[k], need[k])
                        waited[k] = need[k]
                ins = o.fn(eng)
                if o.chan is not None:
                    ins.then_inc(sems[o.chan], o.inc)
                elif o.sig:
                    ins.then_inc(sems[eng_name], 1)
            if eng_name == "sp":
                for s, v in final_waits:
                    eng.wait_ge(s, v)

        with nc.Block() as block:
            @block.tensor
            def _(e):
                run("pe", e)

            @block.scalar
            def _(e):
                run("act", e)

            @block.vector
            def _(e):
                run("dve", e)

            @block.gpsimd
            def _(e):
                run("pool", e)

            @block.sync
            def _(e):
                run("sp", e)


def build(c):
    nc = bass.Bass("TRN2", target_bir_lowering=False)
    es = ExitStack()
    P = Prog()
    D, DC, T, SC, XC = c.D, c.DC, c.T, c.SC, c.XC

    def dram(name, shape, dt, kind):
        return nc.dram_tensor(name, list(shape), dt, kind=kind).ap()

    xT = dram("xT", [D, c.NTOKH], F32, "ExternalInput")
    memT = dram("memT", [D, c.MEM], F32, "ExternalInput")
    pos = dram("pos", [128, c.NTOKH], I32, "ExternalInput")
    consts = dram("consts", [128, c.NCONST], F32, "ExternalInput")
    invf = dram("invf", [128, c.NG * 16], F32, "ExternalInput")
    outT = dram("outT", [D, c.TOK], F32, "ExternalOutput")
    wsh, wshb, wfull = {}, {}, {}
    for name, U in c.PIECES:
        wsh[name] = dram("w_" + name, [U * 128, D], F32, "ExternalInput")

    def sb(name, shape, dt):
        return es.enter_context(nc.sbuf_tensor(name, list(shape), dt))

    h = sb("h", [128, DC, T], F32)
    A = sb("A", [128, DC, T], BF16)
    B = sb("B", [128, DC, T], BF16)
    NDED = 2
    ring = sb("ring", [128, NDED, D], BF16)
    FT = [sb("ft%d" % i, [128, 16 + T], F32) for i in range(4)]
    uhist = sb("uhist", [128, SC, 16], F32)
    kT = sb("kT", [128, c.NKV, 128 + T], BF16)
    vtm = sb("vtm", [128, 1 + T // 128, c.NKV * 128], BF16)
    ET = [sb("et%d" % i, [128, 512], BF16) for i in range(4)]
    SQ = [sb("sq%d" % i, [128, T], BF16) for i in range(2)]
    hT = sb("hT", [128, 2 * c.FC, T], BF16)
    KmT = sb("KmT", [128, 2 * XC, c.MEM], BF16)
    Vm = sb("Vm", [128, 2 * c.MC, c.XW], BF16)
    cosT = sb("cosT", [128, T], F32)
    sinT = sb("sinT", [128, T], F32)
    posi = sb("posi", [128, T], I32)
    kint = sb("kint", [128, T], I32)
    cst = sb("cst", [128, c.NCONST], F32)
    mk = sb("mk", [128, 384], BF16)
    rm = sb("rm", [128, 128], BF16)
    ones = sb("ones", [128, 128], BF16)
    esink = sb("esink", [128, c.NQ], F32)
    epst = sb("epst", [128, 1], F32)
    pst = [es.enter_context(nc.psum_tensor("ps%d" % i, [128, 512], F32)) for i in range(8)]

    Hr = [Region() for _ in range(DC)]
    Ar = [Region() for _ in range(DC)]
    Br = [Region() for _ in range(DC)]
    Sr = [Region() for _ in range(NDED)]
    FTr = [Region() for _ in range(4)]
    UHr = [Region() for _ in range(SC)]
    Kprev = [Region() for _ in range(c.NKV)]
    Kcur = [Region() for _ in range(c.NKV)]
    Vr = [Region() for _ in range(1 + T // 128)]
    ETr = [Region() for _ in range(4)]
    SQr = [Region() for _ in range(2)]
    HTr = [Region() for _ in range(2 * c.FC)]
    KmR, VmR = Region(), Region()
    CSr, TRIGr, POSr, KIr = Region(), Region(), Region(), Region()
    PSr = [Region() for _ in range(8)]
    Wpiece = {name: Region() for name, _ in c.PIECES}
    state = SimpleNamespace(ps=0, ft=0, et=0, sq=0, slot=0, nslots=NDED, och=0, nload=0)

    def next_ps():
        i = state.ps; state.ps = (i + 1) % 8
        return i

    def next_ft():
        i = state.ft; state.ft = (i + 1) % 4
        return i

    def next_et():
        i = state.et; state.et = (i + 1) % 4
        return i

    def next_sq():
        i = state.sq; state.sq = (i + 1) % 2
        return i

    def slot_ap(s):
        if s < NDED:
            return ring[:, s, :]
        k = s - NDED
        q = DC // 4
        return B[:, k * q:(k + 1) * q, :].rearrange("p a b -> p (a b)")

    def slot_regions(s):
        if s < NDED:
            return [Sr[s]]
        k = s - NDED
        q = DC // 4
        return Br[k * q:(k + 1) * q]

    def load_unit(piece, u):
        s = state.slot % state.nslots
        state.slot += 1
        ap = slot_ap(s)
        regs = slot_regions(s)
        phys, off = c.LOG[piece]
        src = wsh[phys][(off + u) * 128:(off + u + 1) * 128, :]
        state.nload += 1
        LQ = getattr(c, "LQ", 2)
        lq = "pool"
        P.op(lq, lambda e, ap=ap, src=src: e.dma_start(out=ap, in_=src),
             reads=[Wpiece[phys]], writes=regs, chan="slot%d" % s)
        return ap, regs

    def unit3(ap):
        return ap.rearrange("p (k m) -> p k m", m=128)

    def mm_group(ps_i, out_ap, pairs, reads):
        n = len(pairs)

        def fn(e):
            ins = None
            for i, (l, r) in enumerate(pairs):
                ins = e.matmul(out_ap, l, r, start=(i == 0), stop=(i == n - 1))
            return ins
        return P.op("pe", fn, reads=reads, writes=[PSr[ps_i]])

    def rmsnorm_to_A(gcol, Tt, nchunks=DC):
        ps_i = next_ps()
        for cc in range(nchunks):
            si = next_sq()
            P.op("act", lambda e, cc=cc, si=si: e.activation(SQ[si][:, :Tt], h[:, cc, :Tt], AF.Square),
                 reads=[Hr[cc]], writes=[SQr[si]])
            P.op("pe", lambda e, cc=cc, si=si: e.matmul(pst[ps_i][:, :Tt], ones[:, :], SQ[si][:, :Tt],
                                                      start=(cc == 0), stop=(cc == nchunks - 1)),
                 reads=[SQr[si], CSr], writes=[PSr[ps_i]])
        f1 = next_ft()
        P.op("act", lambda e: e.activation(FT[f1][:, :Tt], pst[ps_i][:, :Tt], AF.Sqrt,
                                           bias=epst[:, 0:1], scale=1.0 / D),
             reads=[PSr[ps_i], CSr], writes=[FTr[f1]])
        f2 = next_ft()
        P.op("dve", lambda e: e.reciprocal(FT[f2][:, :Tt], FT[f1][:, :Tt]), reads=[FTr[f1]], writes=[FTr[f2]])
        for cc in range(nchunks):
            P.op("dve", lambda e, cc=cc: e.scalar_tensor_tensor(
                out=A[:, cc, :Tt], in0=h[:, cc, :Tt], scalar=cst[:, gcol + cc:gcol + cc + 1],
                in1=FT[f2][:, :Tt], op0=ALU.mult, op1=ALU.mult),
                reads=[Hr[cc], FTr[f2], CSr], writes=[Ar[cc]])
        return f2

    def proj_unit(piece, u, Tt, kc_n=DC, src=A, src_r=Ar):
        ap, regs = load_unit(piece, u)
        u3 = unit3(ap)
        ps_i = next_ps()
        mm_group(ps_i, pst[ps_i][:, :Tt], [(u3[:, kc, :], src[:, kc, :Tt]) for kc in range(kc_n)],
                 reads=regs + src_r[:kc_n])
        return ps_i

    def out_proj(piece, u0, Tt):
        for dc in range(DC):
            ps_i = proj_unit(piece, u0 + dc, Tt)
            P.op("dve", lambda e, dc=dc, ps_i=ps_i: e.tensor_tensor(out=h[:, dc, :Tt], in0=h[:, dc, :Tt],
                                                                    in1=pst[ps_i][:, :Tt], op=ALU.add),
                 reads=[PSr[ps_i], Hr[dc]], writes=[Hr[dc]])

    def mlp(layer, Tt):
        rmsnorm_to_A(c.cG + (4 + layer) * DC, Tt)
        state.nslots = NDED + 4
        state.slot = 0
        w1, w2 = "w1_%d" % layer, "w2_%d" % layer
        NS = c.FCH // c.FC

        def phaseA(s):
            par = s % 2
            for f in range(c.FC):
                ps_i = proj_unit(w1, s * c.FC + f, Tt)
                fi = next_ft()
                P.op("act", lambda e, ps_i=ps_i, fi=fi: e.activation(FT[fi][:, :Tt], pst[ps_i][:, :Tt], AF.Relu),
                     reads=[PSr[ps_i]], writes=[FTr[fi]])
                hi = par * c.FC + f
                P.op("act", lambda e, fi=fi, hi=hi: e.activation(hT[:, hi, :Tt], FT[fi][:, :Tt], AF.Square),
                     reads=[FTr[fi]], writes=[HTr[hi]])

        def phaseB(s):
            par = s % 2
            units = [load_unit(w2, s * c.FC + f) for f in range(c.FC)]
            regs = [r for _, rs in units for r in rs]
            for dc in range(DC):
                ps_i = next_ps()
                mm_group(ps_i, pst[ps_i][:, :Tt],
                         [(units[f][0][:, dc * 128:(dc + 1) * 128], hT[:, par * c.FC + f, :Tt]) for f in range(c.FC)],
                         reads=regs + HTr[par * c.FC:(par + 1) * c.FC])
                P.op("dve", lambda e, dc=dc, ps_i=ps_i: e.tensor_tensor(out=h[:, dc, :Tt], in0=h[:, dc, :Tt],
                                                                        in1=pst[ps_i][:, :Tt], op=ALU.add),
                     reads=[PSr[ps_i], Hr[dc]], writes=[Hr[dc]])

        phaseA(0)
        for s in range(NS):
            if s + 1 < NS:
                phaseA(s + 1)
            phaseB(s)
        state.nslots = NDED
        state.slot = 0

    def cross_attn(layer, Tt):
        sc = float(c.XD) ** -0.5
        for hx in range(c.XH):
            ets = []
            for mc in range(c.MC):
                ps_i = next_ps()
                mm_group(ps_i, pst[ps_i][:, :Tt],
                         [(KmT[:, layer * XC + hx * c.XHC + dc, mc * 128:(mc + 1) * 128], B[:, SC + hx * c.XHC + dc, :Tt])
                          for dc in range(c.XHC)],
                         reads=[KmR] + Br[SC + hx * c.XHC: SC + (hx + 1) * c.XHC])
                ei = next_et()
                P.op("act", lambda e, ps_i=ps_i, ei=ei: e.activation(ET[ei][:, :Tt], pst[ps_i][:, :Tt], AF.Exp, scale=sc),
                     reads=[PSr[ps_i]], writes=[ETr[ei]])
                ets.append(ei)
            pd = next_ps()
            mm_group(pd, pst[pd][:, :Tt], [(ones[:, :], ET[ei][:, :Tt]) for ei in ets],
                     reads=[ETr[ei] for ei in ets] + [CSr])
            fi = next_ft()
            P.op("dve", lambda e, pd=pd, fi=fi: e.reciprocal(FT[fi][:, :Tt], pst[pd][:, :Tt]),
                 reads=[PSr[pd]], writes=[FTr[fi]])
            for dc in range(c.XHC):
                po = next_ps()
                col = (hx * c.XHC + dc) * 128
                mm_group(po, pst[po][:, :Tt],
                         [(Vm[:, layer * c.MC + mc, col:col + 128], ET[ets[mc]][:, :Tt]) for mc in range(c.MC)],
                         reads=[VmR] + [ETr[ei] for ei in ets])
                yc = SC + hx * c.XHC + dc
                P.op("dve", lambda e, po=po, fi=fi, yc=yc: e.tensor_tensor(out=A[:, yc, :Tt], in0=pst[po][:, :Tt],
                                                                           in1=FT[fi][:, :Tt], op=ALU.mult),
                     reads=[PSr[po], FTr[fi]], writes=[Ar[yc]])

    def mixer0(Tt, first_real):
        rmsnorm_to_A(c.cG + 0 * DC, Tt)
        for j in range(DC):
            ps_i = proj_unit("mix0", j, Tt)
            if j >= SC:
                P.op("act", lambda e, j=j, ps_i=ps_i: e.activation(B[:, j, :Tt], pst[ps_i][:, :Tt], AF.Copy),
                     reads=[PSr[ps_i]], writes=[Br[j]])
                continue
            g = j // c.GC
            w = c.WINDOWS[g]
            ui = next_ft()
            ub = FT[ui]
            P.op("dve", lambda e, ub=ub, j=j: e.tensor_copy(ub[:, 0:16], uhist[:, j, :]),
                 reads=[UHr[j]], writes=[FTr[ui]])
            P.op("act", lambda e, ub=ub, ps_i=ps_i: e.activation(ub[:, 16:16 + Tt], pst[ps_i][:, :Tt], AF.Copy),
                 reads=[PSr[ps_i], FTr[ui]], writes=[FTr[ui]])
            P.op("dve", lambda e, ub=ub, j=j: e.tensor_copy(uhist[:, j, :], ub[:, Tt:Tt + 16]),
                 reads=[FTr[ui]], writes=[UHr[j]])
            cur, curi = ub, ui
            step = 1
            W = 16 + Tt
            scratch = [next_ft(), next_ft()]
            nstep = 0
            while step < w:
                ni = scratch[nstep % 2]
                nstep += 1
                nx = FT[ni]
                lo = 2 * step - 1
                P.op("dve", lambda e, nx=nx, cur=cur, lo=lo, step=step: e.tensor_tensor(
                    out=nx[:, lo:W], in0=cur[:, lo:W], in1=cur[:, lo - step:W - step], op=ALU.add),
                    reads=[FTr[curi]], writes=[FTr[ni]])
                cur, curi = nx, ni
                step *= 2
            P.op("dve", lambda e, cur=cur, ub=ub, j=j, w=w: e.scalar_tensor_tensor(
                out=B[:, j, :Tt], in0=cur[:, 16:16 + Tt], scalar=1.0 / w, in1=ub[:, 16:16 + Tt],
                op0=ALU.mult, op1=ALU.subtract),
                reads=[FTr[curi], FTr[ui]], writes=[Br[j]])
            if first_real:
                ti = next_ft()
                P.op("dve", lambda e, ti=ti, cur=cur, g=g: e.tensor_tensor(
                    out=FT[ti][:, :16], in0=cur[:, 16:32], in1=cst_inv[:, g * 16:(g + 1) * 16], op=ALU.mult),
                    reads=[FTr[curi], CIr], writes=[FTr[ti]])
                P.op("dve", lambda e, ti=ti, ub=ub, j=j: e.tensor_tensor(
                    out=B[:, j, :16], in0=FT[ti][:, :16], in1=ub[:, 16:32], op=ALU.subtract),
                    reads=[FTr[ti], FTr[ui]], writes=[Br[j]])
        SUB = getattr(c, "SUB", 99)
        if SUB <= 1:
            return
        for g in range(c.NG):
            for oc in range(c.GC):
                ap, regs = load_unit("mix0", DC + g * c.GC + oc)
                u3 = unit3(ap)
                ps_i = next_ps()
                mm_group(ps_i, pst[ps_i][:, :Tt],
                         [(u3[:, ic, :], B[:, g * c.GC + ic, :Tt]) for ic in range(c.GC)],
                         reads=regs + Br[g * c.GC:(g + 1) * c.GC])
                yc = g * c.GC + oc
                P.op("act", lambda e, yc=yc, ps_i=ps_i: e.activation(A[:, yc, :Tt], pst[ps_i][:, :Tt], AF.Identity,
                                                                     scale=cst[:, c.cPS + yc:c.cPS + yc + 1]),
                     reads=[PSr[ps_i], CSr], writes=[Ar[yc]])
        if SUB <= 2:
            return
        cross_attn(0, Tt)
        if SUB <= 3:
            return
        out_proj("mix0", DC + c.NG * c.GC, Tt)

    def rope_tables(t0, Tt):
        P.op("pool", lambda e: e.dma_start(out=posi[:, :Tt], in_=pos[:, t0:t0 + Tt]), reads=[], writes=[POSr], chan="pos")
        fa = next_ft()
        P.op("dve", lambda e: e.tensor_copy(FT[fa][:, :Tt], posi[:, :Tt]), reads=[POSr], writes=[FTr[fa]])
        P.op("dve", lambda e: e.tensor_scalar(FT[fa][:, :Tt], FT[fa][:, :Tt], cst[:, c.cIF:c.cIF + 1], None,
                                              op0=ALU.mult),
             reads=[FTr[fa], CSr], writes=[FTr[fa]])
        two_pi = 2.0 * np.pi

        def reduced_sin(shift, dst):
            MAGIC = 12582912.0
            fb = next_ft()
            P.op("dve", lambda e: e.tensor_scalar(FT[fb][:, :Tt], FT[fa][:, :Tt], shift, 1.0 / two_pi,
                                                  op0=ALU.add, op1=ALU.mult),
                 reads=[FTr[fa]], writes=[FTr[fb]])
            P.op("dve", lambda e: e.tensor_scalar(FT[fb][:, :Tt], FT[fb][:, :Tt], MAGIC, None, op0=ALU.add),
                 reads=[FTr[fb]], writes=[FTr[fb]])
            P.op("dve", lambda e: e.tensor_scalar(FT[fb][:, :Tt], FT[fb][:, :Tt], -MAGIC, None, op0=ALU.add),
                 reads=[FTr[fb]], writes=[FTr[fb]])
            fc = next_ft()
            P.op("dve", lambda e: e.tensor_scalar(FT[fc][:, :Tt], FT[fa][:, :Tt], shift, None, op0=ALU.add),
                 reads=[FTr[fa]], writes=[FTr[fc]])
            P.op("dve", lambda e: e.scalar_tensor_tensor(out=FT[fc][:, :Tt], in0=FT[fb][:, :Tt], scalar=-two_pi,
                                                         in1=FT[fc][:, :Tt], op0=ALU.mult, op1=ALU.add),
                 reads=[FTr[fb], FTr[fc]], writes=[FTr[fc]])
            P.op("dve", lambda e: e.tensor_scalar(FT[fc][:, :Tt], FT[fc][:, :Tt], -3.1415925, 3.1415925,
                                                  op0=ALU.max, op1=ALU.min),
                 reads=[FTr[fc]], writes=[FTr[fc]])
            P.op("act", lambda e: e.activation(dst[:, :Tt], FT[fc][:, :Tt], AF.Sin), reads=[FTr[fc]], writes=[TRIGr])

        reduced_sin(0.0, sinT)
        reduced_sin(0.5 * np.pi, cosT)

    def rope_evac(ps_i, dst_ap, dst_regs, Tt):
        P.op("act", lambda e: e.activation(dst_ap, pst[ps_i][:, :Tt], AF.Copy), reads=[PSr[ps_i]], writes=dst_regs)
        if getattr(c, "NOROT", 0) == 1:
            return
        pr = next_ps()
        rv = getattr(c, "ROTV", 0)
        l_ap = ones[:, :] if rv == 2 else rm[:, :]
        r_ap = A[:, 0, :Tt] if rv in (1, 3) else dst_ap
        P.op("pe", lambda e: e.matmul(pst[pr][:, :Tt], l_ap, r_ap, start=True, stop=True),
             reads=([CSr] if rv == 3 else dst_regs + [CSr]), writes=[PSr[pr]])
        if getattr(c, "NOROT", 0) == 2:
            return
        f1 = next_ft()
        P.op("dve", lambda e: e.tensor_tensor(out=FT[f1][:, :Tt], in0=pst[ps_i][:, :Tt], in1=cosT[:, :Tt], op=ALU.mult),
             reads=[PSr[ps_i], TRIGr] + list(dst_regs), writes=[FTr[f1]])
        if getattr(c, "NOROT", 0) == 3:
            return
        f2 = next_ft()
        P.op("dve", lambda e: e.tensor_tensor(out=FT[f2][:, :Tt], in0=pst[pr][:, :Tt], in1=sinT[:, :Tt], op=ALU.mult),
             reads=[PSr[pr], TRIGr], writes=[FTr[f2]])
        P.op("dve", lambda e: e.tensor_tensor(out=dst_ap, in0=FT[f1][:, :Tt], in1=FT[f2][:, :Tt], op=ALU.add),
             reads=[FTr[f1], FTr[f2]], writes=dst_regs)

    def mixer1(Tt, t0, halo, first_real):
        rmsnorm_to_A(c.cG + 1 * DC, Tt)
        rope_tables(t0, Tt)
        nb = Tt // 128
        SUB1 = getattr(c, "SUB1", 99)
        if SUB1 <= 1:
            return
        if not halo:
            for j in range(SC):
                ps_i = proj_unit("mix1", j, Tt)
                rope_evac(ps_i, B[:, j, :Tt], [Br[j]], Tt)
        for g in range(c.NKV):
            ps_i = proj_unit("mix1", SC + g, Tt)
            rope_evac(ps_i, kT[:, g, 128:128 + Tt], [Kcur[g]], Tt)
        if SUB1 <= 2:
            return
        for g in range(c.NKV):
            ap, regs = load_unit("mix1", SC + c.NKV + g)
            u3 = unit3(ap)
            ps_i = next_ps()
            for blk in range(nb):
                mm_group(ps_i, pst[ps_i][:, blk * 128:(blk + 1) * 128],
                         [(A[:, kc, blk * 128:(blk + 1) * 128], u3[:, kc, :]) for kc in range(DC)],
                         reads=regs + Ar)
            P.op("act", lambda e, g=g, ps_i=ps_i: e.activation(
                vtm[:, 1:1 + nb, g * 128:(g + 1) * 128],
                pst[ps_i][:, :nb * 128].rearrange("p (b m) -> p b m", m=128), AF.Copy),
                reads=[PSr[ps_i]], writes=Vr[1:1 + nb])
        if not halo:
            for i in range(XC):
                ps_i = proj_unit("mix1", SC + 2 * c.NKV + i, Tt)
                P.op("act", lambda e, i=i, ps_i=ps_i: e.activation(B[:, SC + i, :Tt], pst[ps_i][:, :Tt], AF.Copy),
                     reads=[PSr[ps_i]], writes=[Br[SC + i]])
            sc = 128.0 ** -0.5
            HB = c.HB
            NW = HB * 128
            for g in range(c.NKV):
                for blk in range(nb):
                    for half in range(c.GQ // HB):
                        h0 = g * c.GQ + half * HB
                        qap = B[:, h0:h0 + HB, blk * 128:(blk + 1) * 128]
                        qregs = Br[h0:h0 + HB]
                        ets = []
                        for which in range(2):
                            kap = kT[:, g, blk * 128 + which * 128: blk * 128 + which * 128 + 128]
                            kreg = [Kprev[g], Kcur[g]] if blk == 0 and which == 0 else [Kcur[g]]
                            ps_i = next_ps()
                            P.op("pe", lambda e, ps_i=ps_i, kap=kap, qap=qap: e.matmul(
                                pst[ps_i][:, :NW].rearrange("p (a b) -> p a b", b=128), kap, qap, start=True, stop=True),
                                reads=kreg + qregs, writes=[PSr[ps_i]])
                            ei = next_et()
                            P.op("act", lambda e, ps_i=ps_i, ei=ei: e.activation(ET[ei][:, :NW], pst[ps_i][:, :NW], AF.Exp, scale=sc),
                                 reads=[PSr[ps_i]], writes=[ETr[ei]])
                            if which == 1:
                                mcol = 0
                            else:
                                mcol = 256 if (first_real and blk == 0) else 128
                            mt = mk[:, mcol:mcol + 128]
                            mb = bass.AP(mt.tensor, mt.offset, [mt.ap[0], [0, HB], mt.ap[1]])
                            P.op("dve", lambda e, ei=ei, mb=mb: e.tensor_tensor(
                                out=ET[ei][:, :NW].rearrange("p (a b) -> p a b", b=128),
                                in0=ET[ei][:, :NW].rearrange("p (a b) -> p a b", b=128), in1=mb, op=ALU.mult),
                                reads=[ETr[ei], CSr], writes=[ETr[ei]])
                            ets.append(ei)
                        pd = next_ps()
                        mm_group(pd, pst[pd][:, :NW], [(ones[:, :], ET[ei][:, :NW]) for ei in ets],
                                 reads=[ETr[ei] for ei in ets] + [CSr])
                        po = next_ps()
                        mm_group(po, pst[po][:, :NW],
                                 [(vtm[:, blk + which, g * 128:(g + 1) * 128], ET[ets[which]][:, :NW]) for which in range(2)],
                                 reads=[ETr[ei] for ei in ets] + [Vr[blk], Vr[blk + 1]])
                        fi = next_ft()
                        st = esink[:, h0:h0 + HB]
                        sbc = bass.AP(st.tensor, st.offset, [st.ap[0], st.ap[1], [0, 128]])
                        P.op("dve", lambda e, pd=pd, fi=fi, sbc=sbc: e.tensor_tensor(
                            out=FT[fi][:, :NW].rearrange("p (a b) -> p a b", b=128),
                            in0=pst[pd][:, :NW].rearrange("p (a b) -> p a b", b=128), in1=sbc, op=ALU.add),
                            reads=[PSr[pd], CSr], writes=[FTr[fi]])
                        P.op("dve", lambda e, fi=fi: e.reciprocal(FT[fi][:, :NW], FT[fi][:, :NW]),
                             reads=[FTr[fi]], writes=[FTr[fi]])
                        P.op("dve", lambda e, po=po, fi=fi, h0=h0, blk=blk: e.tensor_tensor(
                            out=A[:, h0:h0 + HB, blk * 128:(blk + 1) * 128],
                            in0=pst[po][:, :NW].rearrange("p (a b) -> p a b", b=128),
                            in1=FT[fi][:, :NW].rearrange("p (a b) -> p a b", b=128), op=ALU.mult),
                            reads=[PSr[po], FTr[fi]], writes=Ar[h0:h0 + HB])
        if SUB1 <= 3:
            return
        for g in range(c.NKV):
            P.op("dve", lambda e, g=g: e.tensor_copy(kT[:, g, 0:128], kT[:, g, Tt:Tt + 128]),
                 reads=[Kcur[g]], writes=[Kprev[g]])
        P.op("dve", lambda e: e.tensor_copy(vtm[:, 0, :], vtm[:, nb, :]), reads=[Vr[nb]], writes=[Vr[0]])
        if not halo:
            cross_attn(1, Tt)
            out_proj("mix1", c.U1, Tt)

    cst_inv = sb("cst_inv", [128, c.NG * 16], F32)
    P.op("sp", lambda e: e.dma_start(out=cst[:, :], in_=consts), writes=[CSr], chan="cst")
    CIr = Region()
    P.op("sp", lambda e: e.dma_start(out=cst_inv[:, :], in_=invf), writes=[CIr], chan="cst2")
    P.op("dve", lambda e: e.tensor_copy(mk[:, :], cst[:, c.cMK:c.cMK + 384]), reads=[CSr], writes=[CSr])
    P.op("dve", lambda e: e.tensor_copy(rm[:, :], cst[:, c.cRM:c.cRM + 128]), reads=[CSr], writes=[CSr])
    P.op("dve", lambda e: e.memset(ones[:, :], 1.0), writes=[CSr])
    P.op("dve", lambda e: e.memset(epst[:, :], EPS), writes=[CSr])
    P.op("dve", lambda e: e.memset(uhist[:, :, :], 0.0), writes=UHr)
    P.op("dve", lambda e: e.memset(kT[:, :, :], 0.0), writes=Kprev + Kcur)
    P.op("dve", lambda e: e.memset(vtm[:, :, :], 0.0), writes=Vr)
    P.op("act", lambda e: e.activation(esink[:, :], cst[:, c.cSK:c.cSK + c.NQ], AF.Exp), reads=[CSr], writes=[CSr])
    STAGE = getattr(c, "STAGE", 99)
    for q4 in range(4 if STAGE >= 2 else 0):
        q = DC // 4
        P.op("sp", lambda e, q4=q4, q=q: e.dma_start(
            out=h[:, q4 * q:(q4 + 1) * q, :c.MEM],
            in_=memT[q4 * q * 128:(q4 + 1) * q * 128, :].rearrange("(c p) t -> p c t", p=128)),
            writes=Hr[q4 * q:(q4 + 1) * q], chan="x%d" % q4)
    for layer in range(2 if STAGE >= 2 else 0):
        rmsnorm_to_A(c.cG + (2 + layer) * DC, c.MEM)
        for j in range(XC):
            ps_i = proj_unit("kv", layer * 2 * XC + j, c.MEM)
            P.op("act", lambda e, j=j, ps_i=ps_i, layer=layer: e.activation(KmT[:, layer * XC + j, :], pst[ps_i][:, :c.MEM], AF.Copy),
                 reads=[PSr[ps_i]], writes=[KmR])
        for j in range(XC):
            ap, regs = load_unit("kv", layer * 2 * XC + XC + j)
            u3 = unit3(ap)
            ps_i = next_ps()
            for mc in range(c.MC):
                mm_group(ps_i, pst[ps_i][:, mc * 128:(mc + 1) * 128],
                         [(A[:, kc, mc * 128:(mc + 1) * 128], u3[:, kc, :]) for kc in range(DC)],
                         reads=regs + Ar)
            P.op("act", lambda e, j=j, ps_i=ps_i, layer=layer: e.activation(
                Vm[:, layer * c.MC:(layer + 1) * c.MC, j * 128:(j + 1) * 128],
                pst[ps_i][:, :c.MC * 128].rearrange("p (b m) -> p b m", m=128), AF.Copy),
                reads=[PSr[ps_i]], writes=[VmR])

    tiles = [(0, c.HALO, True)] + [(c.HALO + i * T, T, False) for i in range(c.NTILE)]
    for ti, (t0, Tt, halo) in enumerate(tiles if STAGE >= 3 else []):
        first_real = (ti == 1)
        for q4 in range(4):
            q = DC // 4
            P.op("pool", lambda e, q4=q4, q=q, t0=t0, Tt=Tt: e.dma_start(
                out=h[:, q4 * q:(q4 + 1) * q, :Tt],
                in_=xT[q4 * q * 128:(q4 + 1) * q * 128, t0:t0 + Tt].rearrange("(c p) t -> p c t", p=128)),
                writes=Hr[q4 * q:(q4 + 1) * q], chan="x%d" % q4)
        DBG = getattr(c, "DBG", -1)

        def dump_h():
            o0 = t0 - c.HALO
            for q4 in range(4):
                q = DC // 4
                P.op("pool", lambda e, q4=q4, q=q, o0=o0, Tt=Tt: e.dma_start(
                    out=outT[q4 * q * 128:(q4 + 1) * q * 128, o0:o0 + Tt].rearrange("(c p) t -> p c t", p=128),
                    in_=h[:, q4 * q:(q4 + 1) * q, :Tt]),
                    reads=Hr[q4 * q:(q4 + 1) * q], chan="out%d" % q4)
        if DBG == 0 and not halo:
            dump_h(); continue
        mixer0(Tt, first_real)
        if DBG == 1 and not halo:
            dump_h(); continue
        if STAGE == 3:
            break
        mlp(0, Tt)
        if STAGE == 4:
            if getattr(c, "ROTV", 0) == 9:
                mlp(0, Tt)
            break
        if DBG == 2 and not halo:
            dump_h(); continue
        mixer1(Tt, t0, halo, first_real)
        if DBG == 3 and not halo:
            dump_h(); continue
        if STAGE == 5:
            break
        if halo:
            continue
        mlp(1, Tt)
        ps_i = next_ps()
        for cc in range(DC):
            si = next_sq()
            P.op("act", lambda e, cc=cc, si=si: e.activation(SQ[si][:, :Tt], h[:, cc, :Tt], AF.Square),
                 reads=[Hr[cc]], writes=[SQr[si]])
            P.op("pe", lambda e, cc=cc, si=si, ps_i=ps_i: e.matmul(pst[ps_i][:, :Tt], ones[:, :], SQ[si][:, :Tt],
                                                                  start=(cc == 0), stop=(cc == DC - 1)),
                 reads=[SQr[si], CSr], writes=[PSr[ps_i]])
        f1 = next_ft()
        P.op("act", lambda e, f1=f1, ps_i=ps_i: e.activation(FT[f1][:, :Tt], pst[ps_i][:, :Tt], AF.Sqrt,
                                                             bias=epst[:, 0:1], scale=1.0 / D),
             reads=[PSr[ps_i], CSr], writes=[FTr[f1]])
        f2 = next_ft()
        P.op("dve", lambda e, f1=f1, f2=f2: e.reciprocal(FT[f2][:, :Tt], FT[f1][:, :Tt]), reads=[FTr[f1]], writes=[FTr[f2]])
        gcol = c.cG + 6 * DC
        for cc in range(DC):
            P.op("dve", lambda e, cc=cc, f2=f2: e.scalar_tensor_tensor(
                out=h[:, cc, :Tt], in0=h[:, cc, :Tt], scalar=cst[:, gcol + cc:gcol + cc + 1],
                in1=FT[f2][:, :Tt], op0=ALU.mult, op1=ALU.mult),
                reads=[Hr[cc], FTr[f2], CSr], writes=[Hr[cc]])
        o0 = t0 - c.HALO
        for q4 in range(4):
            q = DC // 4
            P.op("pool", lambda e, q4=q4, q=q, o0=o0, Tt=Tt: e.dma_start(
                out=outT[q4 * q * 128:(q4 + 1) * q * 128, o0:o0 + Tt].rearrange("(c p) t -> p c t", p=128),
                in_=h[:, q4 * q:(q4 + 1) * q, :Tt]),
                reads=Hr[q4 * q:(q4 + 1) * q], chan="out%d" % q4)

    P.finalize(nc, es)
    es.close()
    return nc


def _units_in(W):
    K, N = W.shape
    KC, U = K // 128, N // 128
    return W.reshape(KC, 128, U, 128).transpose(2, 1, 0, 3).reshape(U, 128, KC * 128)


def _pad_units(u, n, width):
    U, p, w = u.shape
    out = np.zeros((n, 128, width), np.float32)
    out[:U, :, :w] = u
    return out


def _struct_consts(c):
    inv_freq = (ROPE_THETA ** (-np.arange(0, 32, 2, dtype=np.float32) / np.float32(32))).astype(np.float32)
    col = np.zeros((128,), np.float32)
    col[:32] = np.concatenate([inv_freq, inv_freq])
    k = np.arange(128)[:, None]
    q = np.arange(128)[None, :]
    m_cur = (k <= q).astype(np.float32)
    m_prev = (k > q).astype(np.float32)
    rmat = np.zeros((128, 128), np.float32)
    for m in range(16):
        rmat[m + 16, m] = -1.0
        rmat[m, m + 16] = 1.0
    return col, m_cur, m_prev, rmat


def prep_inputs(c, inp):
    f = lambda a: np.asarray(a, dtype=np.float32)
    x, mem, positions = f(inp["x"]), f(inp["mem"]), np.asarray(inp["positions"]).astype(np.int32)
    D, DC = c.D, c.DC
    col, m_cur, m_prev, rmat = _struct_consts(c)
    wkv = f(inp["w_mem_kv"])
    kv_units = np.concatenate([_units_in(wkv[0]), _units_in(wkv[1])], 0)
    wg = f(inp["pool_w_group"])[0]
    wg_units = np.concatenate([_pad_units(_units_in(wg[g]), c.GC, D) for g in range(c.NG)], 0)
    mix0 = np.concatenate([_units_in(f(inp["pool_w_in"])[0]), wg_units, _units_in(f(inp["pool_w_out"])[0])], 0)
    mix1 = np.concatenate([_units_in(f(inp["attn_w_in"])[0]), _units_in(f(inp["attn_w_out"])[0])], 0)
    w1 = f(inp["mlp_w1"]); w2 = f(inp["mlp_w2"])
    pieces = {
        "kv": kv_units, "mix0": mix0, "mix1": mix1,
        "w1_0": _units_in(w1[0]), "w2_0": w2[0].reshape(c.FCH, 128, D),
        "w1_1": _units_in(w1[1]), "w2_1": w2[1].reshape(c.FCH, 128, D),
    }
    shards = {}
    for name, U in c.PIECES:
        u = pieces[name]
        if u.shape[0] != U:
            u = _pad_units(u, U, D)
        u = np.ascontiguousarray(u).reshape(U * 128, D)
        shards[name] = [u for i in range(NCORES)]
    def percol(v):
        return np.ascontiguousarray(v.reshape(-1, 128).T)
    gains = [percol(f(inp["norm_mix"])[0]), percol(f(inp["norm_mix"])[1]),
             percol(f(inp["norm_mem"])[0]), percol(f(inp["norm_mem"])[1]),
             percol(f(inp["norm_mlp"])[0]), percol(f(inp["norm_mlp"])[1]),
             percol(f(inp["final_norm"]))]
    ps = percol(f(inp["pool_scale"])[0])
    sink = np.broadcast_to(f(inp["attn_sink"])[0][None, :], (128, c.NQ))
    in_maps = []
    nq = NCORES // 2
    for core in range(NCORES):
        b, qd = core // nq, core % nq
        s0 = qd * c.TOK
        xs = np.zeros((c.NTOKH, D), np.float32)
        ps_ = np.zeros((c.NTOKH,), np.int32)
        if qd == 0:
            xs[c.HALO:] = x[b, s0:s0 + c.TOK]
            ps_[c.HALO:] = positions[b, s0:s0 + c.TOK]
            m_pf = np.zeros_like(m_prev)
        else:
            xs[:] = x[b, s0 - c.HALO:s0 + c.TOK]
            ps_[:] = positions[b, s0 - c.HALO:s0 + c.TOK]
            m_pf = m_prev
        consts = np.concatenate(gains + [ps, sink, col[:, None], m_cur, m_prev, m_pf, rmat], axis=1).astype(np.float32)
        assert consts.shape == (128, c.NCONST), consts.shape
        inv = np.zeros((c.NG, 16), np.float32)
        for g, w in enumerate(c.WINDOWS):
            inv[g, :] = np.float32(1.0) / np.float32(w)
            if qd == 0:
                n = np.minimum(np.arange(1, 17), w).astype(np.float32)
                inv[g, :] = np.float32(1.0) / n
        m = {
            "xT": np.ascontiguousarray(xs.T),
            "memT": np.ascontiguousarray(mem[b].T),
            "pos": np.ascontiguousarray(np.broadcast_to(ps_[None, :], (128, c.NTOKH))),
            "consts": np.ascontiguousarray(consts),
            "invf": np.ascontiguousarray(np.broadcast_to(inv.reshape(1, -1), (128, c.NG * 16))),
        }
        for name, _ in c.PIECES:
            m["w_" + name] = np.ascontiguousarray(shards[name][core])
        in_maps.append(m)
    return in_maps


def run(c, inp):
    nc = build(c)
    in_maps = prep_inputs(c, inp)
    res = run_bass_kernel_spmd(nc, in_maps, core_ids=list(range(NCORES)))
    nq = NCORES // 2
    B = np.asarray(inp["x"]).shape[0]
    out = np.zeros((B, nq * c.TOK, c.D), np.float32)
    for core in range(NCORES):
        b, qd = core // nq, core % nq
        out[b, qd * c.TOK:(qd + 1) * c.TOK, :] = np.asarray(res.results[core]["outT"]).T
    return out


def kernel(**inputs):
    return run(make_cfg(), inputs)
```

```python
import numpy as np
from contextlib import ExitStack
from types import SimpleNamespace
import concourse.bass as bass
import concourse.mybir as mybir
from concourse.bass_utils import run_bass_kernel_spmd

F32, BF16, I32 = mybir.dt.float32, mybir.dt.bfloat16, mybir.dt.int32
AF = mybir.ActivationFunctionType
ALU = mybir.AluOpType
NCORES = 8
EPS = 1e-6
ROPE_THETA = 500000.0


def make_cfg(D=4096, NG=4, WINDOWS=(2, 4, 8, 16), GQ=8, XH=4, NTILE=4, MEM=256, T=512, HALO=256):
    c = SimpleNamespace()
    c.D = D; c.DC = D // 128; c.SELF = 3 * D // 4; c.SC = c.SELF // 128
    c.XW = D // 4; c.XC = c.XW // 128; c.XH = XH; c.XHC = c.XC // XH; c.XD = c.XW // XH
    c.NG = NG; c.GC = c.SC // NG; c.WINDOWS = tuple(WINDOWS)
    c.NQ = c.SC; c.GQ = GQ; c.NKV = c.NQ // GQ; c.HB = min(4, GQ)
    c.FF = 4 * D; c.FCH = c.FF // 128; c.FC = 2
    c.MEM = MEM; c.MC = MEM // 128; c.T = T; c.HALO = HALO; c.NTILE = NTILE
    c.TOK = NTILE * T; c.NTOKH = HALO + c.TOK
    c.U1 = c.SC + 2 * c.NKV + c.XC
    o = 0
    c.cG = o; o += 7 * c.DC
    c.cPS = o; o += c.SC
    c.cSK = o; o += c.NQ
    c.cIF = o; o += 1
    c.cMK = o; o += 384
    c.cRM = o; o += 128
    c.NCONST = o
    def pad8(n):
        return (n + 7) // 8 * 8
    sizes = [("kv", 4 * c.XC), ("mix0", c.DC + c.NG * c.GC + c.DC), ("w1_0", c.FCH), ("w2_0", c.FCH),
             ("mix1", c.U1 + c.DC), ("w1_1", c.FCH), ("w2_1", c.FCH)]
    c.LOG = {nm: (nm, 0) for nm, n in sizes}
    c.PIECES = [(nm, pad8(n)) for nm, n in sizes]
    return c


class Op:
    __slots__ = ("eng", "fn", "deps", "sig", "chan", "cnt", "inc")


class Region:
    __slots__ = ("w", "r")

    def __init__(self):
        self.w = None
        self.r = {}


class Prog:
    ENGS = ("pe", "act", "dve", "pool", "sp")

    def __init__(self):
        self.q = {e: [] for e in self.ENGS}
        self.chans = {}

    def op(self, eng, fn, reads=(), writes=(), chan=None, inc=16):
        o = Op()
        o.eng = eng; o.fn = fn; o.sig = False; o.chan = chan; o.cnt = 0; o.inc = inc
        deps = set()
        for r in reads:
            if r.w is not None:
                deps.add(r.w)
        for w in writes:
            if w.w is not None:
                deps.add(w.w)
            deps.update(w.r.values())
        if eng == "pe":
            deps = {d for d in deps if not (d.eng == "pe" and d.chan is None)}
        o.deps = deps
        key = chan if chan is not None else eng
        for r in reads:
            r.r[key] = o
        for w in writes:
            w.w = o
            w.r = {}
        self.q[eng].append(o)
        if chan is not None:
            self.chans.setdefault(chan, []).append(o)
        return o

    def finalize(self, nc, es):
        for e in self.ENGS:
            for o in self.q[e]:
                for d in o.deps:
                    d.sig = True
        sems = {}
        totals = {}
        for e in ("pe", "act", "dve"):
            if self.q[e]:
                self.q[e][-1].sig = True
        for e in self.ENGS:
            sems[e] = es.enter_context(nc.semaphore("pg_" + e))
            n = 0
            for o in self.q[e]:
                if o.chan is None and o.sig:
                    n += 1
                    o.cnt = n
            totals[e] = n
        for ch, ops in self.chans.items():
            sems[ch] = es.enter_context(nc.semaphore("ch_" + ch))
            n = 0
            for o in ops:
                n += o.inc
                o.cnt = n
        final_waits = [(sems[ch], ops[-1].cnt) for ch, ops in self.chans.items() if ch.startswith("out") or ch.startswith("ag_")]
        final_waits += [(sems[e], totals[e]) for e in ("pe", "act", "dve") if totals[e] > 0]

        def run(eng_name, eng):
            waited = {}
            for o in self.q[eng_name]:
                need = {}
                for d in o.deps:
                    k = d.chan if d.chan is not None else d.eng
                    if d.cnt > need.get(k, 0):
                        need[k] = d.cnt
                for k in sorted(need):
                    if waited.get(k, 0) < need[k]:
                        eng.wait_ge(sems[k], need[k])
                        waited[k] = need[k]
                ins = o.fn(eng)
                if o.chan is not None:
                    ins.then_inc(sems[o.chan], o.inc)
                elif o.sig:
                    ins.then_inc(sems[eng_name], 1)
            if eng_name == "sp":
                for s, v in final_waits:
                    eng.wait_ge(s, v)

        with nc.Block() as block:
            @block.tensor
            def _(e):
                run("pe", e)

            @block.scalar
            def _(e):
                run("act", e)

            @block.vector
            def _(e):
                run("dve", e)

            @block.gpsimd
            def _(e):
                run("pool", e)

            @block.sync
            def _(e):
                run("sp", e)


def build(c):
    nc = bass.Bass("TRN2", target_bir_lowering=False)
    es = ExitStack()
    P = Prog()
    D, DC, T, SC, XC = c.D, c.DC, c.T, c.SC, c.XC

    def dram(name, shape, dt, kind):
        return nc.dram_tensor(name, list(shape), dt, kind=kind).ap()

    xT = dram("xT", [D, c.NTOKH], F32, "ExternalInput")
    memT = dram("memT", [D, c.MEM], F32, "ExternalInput")
    pos = dram("pos", [128, c.NTOKH], I32, "ExternalInput")
    consts = dram("consts", [128, c.NCONST], F32, "ExternalInput")
    invf = dram("invf", [128, c.NG * 16], F32, "ExternalInput")
    outT = dram("outT", [D, c.TOK], F32, "ExternalOutput")
    wsh, wshb, wfull = {}, {}, {}
    for name, U in c.PIECES:
        wsh[name] = dram("w_" + name, [U * 128, D], F32, "ExternalInput")

    def sb(name, shape, dt):
        return es.enter_context(nc.sbuf_tensor(name, list(shape), dt))

    h = sb("h", [128, DC, T], F32)
    A = sb("A", [128, DC, T], BF16)
    B = sb("B", [128, DC, T], BF16)
    NDED = 2
    ring = sb("ring", [128, NDED, D], BF16)
    FT = [sb("ft%d" % i, [128, 16 + T], F32) for i in range(4)]
    uhist = sb("uhist", [128, SC, 16], F32)
    kT = sb("kT", [128, c.NKV, 128 + T], BF16)
    vtm = sb("vtm", [128, 1 + T // 128, c.NKV * 128], BF16)
    ET = [sb("et%d" % i, [128, 512], BF16) for i in range(4)]
    SQ = [sb("sq%d" % i, [128, T], BF16) for i in range(2)]
    hT = sb("hT", [128, 2 * c.FC, T], BF16)
    KmT = sb("KmT", [128, 2 * XC, c.MEM], BF16)
    Vm = sb("Vm", [128, 2 * c.MC, c.XW], BF16)
    cosT = sb("cosT", [128, T], F32)
    sinT = sb("sinT", [128, T], F32)
    posi = sb("posi", [128, T], I32)
    kint = sb("kint", [128, T], I32)
    cst = sb("cst", [128, c.NCONST], F32)
    mk = sb("mk", [128, 384], BF16)
    rm = sb("rm", [128, 128], BF16)
    ones = sb("ones", [128, 128], BF16)
    esink = sb("esink", [128, c.NQ], F32)
    epst = sb("epst", [128, 1], F32)
    pst = [es.enter_context(nc.psum_tensor("ps%d" % i, [128, 512], F32)) for i in range(8)]

    Hr = [Region() for _ in range(DC)]
    Ar = [Region() for _ in range(DC)]
    Br = [Region() for _ in range(DC)]
    Sr = [Region() for _ in range(NDED)]
    FTr = [Region() for _ in range(4)]
    UHr = [Region() for _ in range(SC)]
    Kprev = [Region() for _ in range(c.NKV)]
    Kcur = [Region() for _ in range(c.NKV)]
    Vr = [Region() for _ in range(1 + T // 128)]
    ETr = [Region() for _ in range(4)]
    SQr = [Region() for _ in range(2)]
    HTr = [Region() for _ in range(2 * c.FC)]
    KmR, VmR = Region(), Region()
    CSr, TRIGr, POSr, KIr = Region(), Region(), Region(), Region()
    PSr = [Region() for _ in range(8)]
    Wpiece = {name: Region() for name, _ in c.PIECES}
    state = SimpleNamespace(ps=0, ft=0, et=0, sq=0, slot=0, nslots=NDED, och=0, nload=0)

    def next_ps():
        i = state.ps; state.ps = (i + 1) % 8
        return i

    def next_ft():
        i = state.ft; state.ft = (i + 1) % 4
        return i

    def next_et():
        i = state.et; state.et = (i + 1) % 4
        return i

    def next_sq():
        i = state.sq; state.sq = (i + 1) % 2
        return i

    def slot_ap(s):
        if s < NDED:
            return ring[:, s, :]
        k = s - NDED
        q = DC // 4
        return B[:, k * q:(k + 1) * q, :].rearrange("p a b -> p (a b)")

    def slot_regions(s):
        if s < NDED:
            return [Sr[s]]
        k = s - NDED
        q = DC // 4
        return Br[k * q:(k + 1) * q]

    def load_unit(piece, u):
        s = state.slot % state.nslots
        state.slot += 1
        ap = slot_ap(s)
        regs = slot_regions(s)
        phys, off = c.LOG[piece]
        src = wsh[phys][(off + u) * 128:(off + u + 1) * 128, :]
        state.nload += 1
        LQ = getattr(c, "LQ", 2)
        lq = "pool"
        P.op(lq, lambda e, ap=ap, src=src: e.dma_start(out=ap, in_=src),
             reads=[Wpiece[phys]], writes=regs, chan="slot%d" % s)
        return ap, regs

    def unit3(ap):
        return ap.rearrange("p (k m) -> p k m", m=128)

    def mm_group(ps_i, out_ap, pairs, reads):
        n = len(pairs)

        def fn(e):
            ins = None
            for i, (l, r) in enumerate(pairs):
                ins = e.matmul(out_ap, l, r, start=(i == 0), stop=(i == n - 1))
            return ins
        return P.op("pe", fn, reads=reads, writes=[PSr[ps_i]])

    def rmsnorm_to_A(gcol, Tt, nchunks=DC):
        ps_i = next_ps()
        for cc in range(nchunks):
            si = next_sq()
            P.op("act", lambda e, cc=cc, si=si: e.activation(SQ[si][:, :Tt], h[:, cc, :Tt], AF.Square),
                 reads=[Hr[cc]], writes=[SQr[si]])
            P.op("pe", lambda e, cc=cc, si=si: e.matmul(pst[ps_i][:, :Tt], ones[:, :], SQ[si][:, :Tt],
                                                      start=(cc == 0), stop=(cc == nchunks - 1)),
                 reads=[SQr[si], CSr], writes=[PSr[ps_i]])
        f1 = next_ft()
        P.op("act", lambda e: e.activation(FT[f1][:, :Tt], pst[ps_i][:, :Tt], AF.Sqrt,
                                           bias=epst[:, 0:1], scale=1.0 / D),
             reads=[PSr[ps_i], CSr], writes=[FTr[f1]])
        f2 = next_ft()
        P.op("dve", lambda e: e.reciprocal(FT[f2][:, :Tt], FT[f1][:, :Tt]), reads=[FTr[f1]], writes=[FTr[f2]])
        for cc in range(nchunks):
            P.op("dve", lambda e, cc=cc: e.scalar_tensor_tensor(
                out=A[:, cc, :Tt], in0=h[:, cc, :Tt], scalar=cst[:, gcol + cc:gcol + cc + 1],
                in1=FT[f2][:, :Tt], op0=ALU.mult, op1=ALU.mult),
                reads=[Hr[cc], FTr[f2], CSr], writes=[Ar[cc]])
        return f2

    def proj_unit(piece, u, Tt, kc_n=DC, src=A, src_r=Ar):
        ap, regs = load_unit(piece, u)
        u3 = unit3(ap)
        ps_i = next_ps()
        mm_group(ps_i, pst[ps_i][:, :Tt], [(u3[:, kc, :], src[:, kc, :Tt]) for kc in range(kc_n)],
                 reads=regs + src_r[:kc_n])
        return ps_i

    def out_proj(piece, u0, Tt):
        state.nslots = NDED + 4
        for dc in range(DC):
            ps_i = proj_unit(piece, u0 + dc, Tt)
            P.op("dve", lambda e, dc=dc, ps_i=ps_i: e.tensor_tensor(out=h[:, dc, :Tt], in0=h[:, dc, :Tt],
                                                                    in1=pst[ps_i][:, :Tt], op=ALU.add),
                 reads=[PSr[ps_i], Hr[dc]], writes=[Hr[dc]])

    def mlp(layer, Tt):
        rmsnorm_to_A(c.cG + (4 + layer) * DC, Tt)
        state.nslots = NDED + 4
        state.slot = 0
        w1, w2 = "w1_%d" % layer, "w2_%d" % layer
        NS = c.FCH // c.FC

        def phaseA(s):
            par = s % 2
            for f in range(c.FC):
                ps_i = proj_unit(w1, s * c.FC + f, Tt)
                fi = next_ft()
                P.op("act", lambda e, ps_i=ps_i, fi=fi: e.activation(FT[fi][:, :Tt], pst[ps_i][:, :Tt], AF.Relu),
                     reads=[PSr[ps_i]], writes=[FTr[fi]])
                hi = par * c.FC + f
                P.op("act", lambda e, fi=fi, hi=hi: e.activation(hT[:, hi, :Tt], FT[fi][:, :Tt], AF.Square),
                     reads=[FTr[fi]], writes=[HTr[hi]])

        def phaseB(s):
            par = s % 2
            units = [load_unit(w2, s * c.FC + f) for f in range(c.FC)]
            regs = [r for _, rs in units for r in rs]
            for dc in range(DC):
                ps_i = next_ps()
                mm_group(ps_i, pst[ps_i][:, :Tt],
                         [(units[f][0][:, dc * 128:(dc + 1) * 128], hT[:, par * c.FC + f, :Tt]) for f in range(c.FC)],
                         reads=regs + HTr[par * c.FC:(par + 1) * c.FC])
                P.op("dve", lambda e, dc=dc, ps_i=ps_i: e.tensor_tensor(out=h[:, dc, :Tt], in0=h[:, dc, :Tt],
                                                                        in1=pst[ps_i][:, :Tt], op=ALU.add),
                     reads=[PSr[ps_i], Hr[dc]], writes=[Hr[dc]])

        phaseA(0)
        for s in range(NS):
            if s + 1 < NS:
                phaseA(s + 1)
            phaseB(s)
        state.nslots = NDED
        state.slot = 0

    def cross_attn(layer, Tt):
        sc = float(c.XD) ** -0.5
        for hx in range(c.XH):
            ets = []
            for mc in range(c.MC):
                ps_i = next_ps()
                mm_group(ps_i, pst[ps_i][:, :Tt],
                         [(KmT[:, layer * XC + hx * c.XHC + dc, mc * 128:(mc + 1) * 128], B[:, SC + hx * c.XHC + dc, :Tt])
                          for dc in range(c.XHC)],
                         reads=[KmR] + Br[SC + hx * c.XHC: SC + (hx + 1) * c.XHC])
                ei = next_et()
                P.op("act", lambda e, ps_i=ps_i, ei=ei: e.activation(ET[ei][:, :Tt], pst[ps_i][:, :Tt], AF.Exp, scale=sc),
                     reads=[PSr[ps_i]], writes=[ETr[ei]])
                ets.append(ei)
            pd = next_ps()
            mm_group(pd, pst[pd][:, :Tt], [(ones[:, :], ET[ei][:, :Tt]) for ei in ets],
                     reads=[ETr[ei] for ei in ets] + [CSr])
            fi = next_ft()
            P.op("dve", lambda e, pd=pd, fi=fi: e.reciprocal(FT[fi][:, :Tt], pst[pd][:, :Tt]),
                 reads=[PSr[pd]], writes=[FTr[fi]])
            for dc in range(c.XHC):
                po = next_ps()
                col = (hx * c.XHC + dc) * 128
                mm_group(po, pst[po][:, :Tt],
                         [(Vm[:, layer * c.MC + mc, col:col + 128], ET[ets[mc]][:, :Tt]) for mc in range(c.MC)],
                         reads=[VmR] + [ETr[ei] for ei in ets])
                yc = SC + hx * c.XHC + dc
                P.op("dve", lambda e, po=po, fi=fi, yc=yc: e.tensor_tensor(out=A[:, yc, :Tt], in0=pst[po][:, :Tt],
                                                                           in1=FT[fi][:, :Tt], op=ALU.mult),
                     reads=[PSr[po], FTr[fi]], writes=[Ar[yc]])

    def mixer0(Tt, first_real):
        rmsnorm_to_A(c.cG + 0 * DC, Tt)
        for j in range(DC):
            ps_i = proj_unit("mix0", j, Tt)
            if j >= SC:
                P.op("act", lambda e, j=j, ps_i=ps_i: e.activation(B[:, j, :Tt], pst[ps_i][:, :Tt], AF.Copy),
                     reads=[PSr[ps_i]], writes=[Br[j]])
                continue
            g = j // c.GC
            w = c.WINDOWS[g]
            ui = next_ft()
            ub = FT[ui]
            P.op("dve", lambda e, ub=ub, j=j: e.tensor_copy(ub[:, 0:16], uhist[:, j, :]),
                 reads=[UHr[j]], writes=[FTr[ui]])
            P.op("act", lambda e, ub=ub, ps_i=ps_i: e.activation(ub[:, 16:16 + Tt], pst[ps_i][:, :Tt], AF.Copy),
                 reads=[PSr[ps_i], FTr[ui]], writes=[FTr[ui]])
            P.op("dve", lambda e, ub=ub, j=j: e.tensor_copy(uhist[:, j, :], ub[:, Tt:Tt + 16]),
                 reads=[FTr[ui]], writes=[UHr[j]])
            cur, curi = ub, ui
            step = 1
            W = 16 + Tt
            scratch = [next_ft(), next_ft()]
            nstep = 0
            while step < w:
                ni = scratch[nstep % 2]
                nstep += 1
                nx = FT[ni]
                lo = 2 * step - 1
                P.op("dve", lambda e, nx=nx, cur=cur, lo=lo, step=step: e.tensor_tensor(
                    out=nx[:, lo:W], in0=cur[:, lo:W], in1=cur[:, lo - step:W - step], op=ALU.add),
                    reads=[FTr[curi]], writes=[FTr[ni]])
                cur, curi = nx, ni
                step *= 2
            P.op("dve", lambda e, cur=cur, ub=ub, j=j, w=w: e.scalar_tensor_tensor(
                out=B[:, j, :Tt], in0=cur[:, 16:16 + Tt], scalar=1.0 / w, in1=ub[:, 16:16 + Tt],
                op0=ALU.mult, op1=ALU.subtract),
                reads=[FTr[curi], FTr[ui]], writes=[Br[j]])
            if first_real:
                ti = next_ft()
                P.op("dve", lambda e, ti=ti, cur=cur, g=g: e.tensor_tensor(
                    out=FT[ti][:, :16], in0=cur[:, 16:32], in1=cst_inv[:, g * 16:(g + 1) * 16], op=ALU.mult),
                    reads=[FTr[curi], CIr], writes=[FTr[ti]])
                P.op("dve", lambda e, ti=ti, ub=ub, j=j: e.tensor_tensor(
                    out=B[:, j, :16], in0=FT[ti][:, :16], in1=ub[:, 16:32], op=ALU.subtract),
                    reads=[FTr[ti], FTr[ui]], writes=[Br[j]])
        SUB = getattr(c, "SUB", 99)
        if SUB <= 1:
            return
        for g in range(c.NG):
            for oc in range(c.GC):
                ap, regs = load_unit("mix0", DC + g * c.GC + oc)
                u3 = unit3(ap)
                ps_i = next_ps()
                mm_group(ps_i, pst[ps_i][:, :Tt],
                         [(u3[:, ic, :], B[:, g * c.GC + ic, :Tt]) for ic in range(c.GC)],
                         reads=regs + Br[g * c.GC:(g + 1) * c.GC])
                yc = g * c.GC + oc
                P.op("act", lambda e, yc=yc, ps_i=ps_i: e.activation(A[:, yc, :Tt], pst[ps_i][:, :Tt], AF.Identity,
                                                                     scale=cst[:, c.cPS + yc:c.cPS + yc + 1]),
                     reads=[PSr[ps_i], CSr], writes=[Ar[yc]])
        if SUB <= 2:
            return
        cross_attn(0, Tt)
        if SUB <= 3:
            return
        out_proj("mix0", DC + c.NG * c.GC, Tt)

    def rope_tables(t0, Tt):
        P.op("pool", lambda e: e.dma_start(out=posi[:, :Tt], in_=pos[:, t0:t0 + Tt]), reads=[], writes=[POSr], chan="pos")
        fa = next_ft()
        P.op("dve", lambda e: e.tensor_copy(FT[fa][:, :Tt], posi[:, :Tt]), reads=[POSr], writes=[FTr[fa]])
        P.op("dve", lambda e: e.tensor_scalar(FT[fa][:, :Tt], FT[fa][:, :Tt], cst[:, c.cIF:c.cIF + 1], None,
                                              op0=ALU.mult),
             reads=[FTr[fa], CSr], writes=[FTr[fa]])
        two_pi = 2.0 * np.pi

        def reduced_sin(shift, dst):
            MAGIC = 12582912.0
            fb = next_ft()
            P.op("dve", lambda e: e.tensor_scalar(FT[fb][:, :Tt], FT[fa][:, :Tt], shift, 1.0 / two_pi,
                                                  op0=ALU.add, op1=ALU.mult),
                 reads=[FTr[fa]], writes=[FTr[fb]])
            P.op("dve", lambda e: e.tensor_scalar(FT[fb][:, :Tt], FT[fb][:, :Tt], MAGIC, None, op0=ALU.add),
                 reads=[FTr[fb]], writes=[FTr[fb]])
            P.op("dve", lambda e: e.tensor_scalar(FT[fb][:, :Tt], FT[fb][:, :Tt], -MAGIC, None, op0=ALU.add),
                 reads=[FTr[fb]], writes=[FTr[fb]])
            fc = next_ft()
            P.op("dve", lambda e: e.tensor_scalar(FT[fc][:, :Tt], FT[fa][:, :Tt], shift, None, op0=ALU.add),
                 reads=[FTr[fa]], writes=[FTr[fc]])
            P.op("dve", lambda e: e.scalar_tensor_tensor(out=FT[fc][:, :Tt], in0=FT[fb][:, :Tt], scalar=-two_pi,
                                                         in1=FT[fc][:, :Tt], op0=ALU.mult, op1=ALU.add),
                 reads=[FTr[fb], FTr[fc]], writes=[FTr[fc]])
            P.op("dve", lambda e: e.tensor_scalar(FT[fc][:, :Tt], FT[fc][:, :Tt], -3.1415925, 3.1415925,
                                                  op0=ALU.max, op1=ALU.min),
                 reads=[FTr[fc]], writes=[FTr[fc]])
            P.op("act", lambda e: e.activation(dst[:, :Tt], FT[fc][:, :Tt], AF.Sin), reads=[FTr[fc]], writes=[TRIGr])

        reduced_sin(0.0, sinT)
        reduced_sin(0.5 * np.pi, cosT)

    def rope_evac(ps_i, dst_ap, dst_regs, Tt):
        P.op("act", lambda e: e.activation(dst_ap, pst[ps_i][:, :Tt], AF.Copy), reads=[PSr[ps_i]], writes=dst_regs)
        if getattr(c, "NOROT", 0) == 1:
            return
        pr = next_ps()
        rv = getattr(c, "ROTV", 0)
        l_ap = ones[:, :] if rv == 2 else rm[:, :]
        r_ap = A[:, 0, :Tt] if rv in (1, 3) else dst_ap
        P.op("pe", lambda e: e.matmul(pst[pr][:, :Tt], l_ap, r_ap, start=True, stop=True),
             reads=([CSr] if rv == 3 else dst_regs + [CSr]), writes=[PSr[pr]])
        if getattr(c, "NOROT", 0) == 2:
            return
        f1 = next_ft()
        P.op("dve", lambda e: e.tensor_tensor(out=FT[f1][:, :Tt], in0=pst[ps_i][:, :Tt], in1=cosT[:, :Tt], op=ALU.mult),
             reads=[PSr[ps_i], TRIGr] + list(dst_regs), writes=[FTr[f1]])
        if getattr(c, "NOROT", 0) == 3:
            return
        f2 = next_ft()
        P.op("dve", lambda e: e.tensor_tensor(out=FT[f2][:, :Tt], in0=pst[pr][:, :Tt], in1=sinT[:, :Tt], op=ALU.mult),
             reads=[PSr[pr], TRIGr], writes=[FTr[f2]])
        P.op("dve", lambda e: e.tensor_tensor(out=dst_ap, in0=FT[f1][:, :Tt], in1=FT[f2][:, :Tt], op=ALU.add),
             reads=[FTr[f1], FTr[f2]], writes=dst_regs)

    def mixer1(Tt, t0, halo, first_real):
        rmsnorm_to_A(c.cG + 1 * DC, Tt)
        rope_tables(t0, Tt)
        nb = Tt // 128
        SUB1 = getattr(c, "SUB1", 99)
        if SUB1 <= 1:
            return
        if not halo:
            for j in range(SC):
                ps_i = proj_unit("mix1", j, Tt)
                rope_evac(ps_i, B[:, j, :Tt], [Br[j]], Tt)
        for g in range(c.NKV):
            ps_i = proj_unit("mix1", SC + g, Tt)
            rope_evac(ps_i, kT[:, g, 128:128 + Tt], [Kcur[g]], Tt)
        if SUB1 <= 2:
            return
        for g in range(c.NKV):
            ap, regs = load_unit("mix1", SC + c.NKV + g)
            u3 = unit3(ap)
            ps_i = next_ps()
            for blk in range(nb):
                mm_group(ps_i, pst[ps_i][:, blk * 128:(blk + 1) * 128],
                         [(A[:, kc, blk * 128:(blk + 1) * 128], u3[:, kc, :]) for kc in range(DC)],
                         reads=regs + Ar)
            P.op("act", lambda e, g=g, ps_i=ps_i: e.activation(
                vtm[:, 1:1 + nb, g * 128:(g + 1) * 128],
                pst[ps_i][:, :nb * 128].rearrange("p (b m) -> p b m", m=128), AF.Copy),
                reads=[PSr[ps_i]], writes=Vr[1:1 + nb])
        if not halo:
            for i in range(XC):
                ps_i = proj_unit("mix1", SC + 2 * c.NKV + i, Tt)
                P.op("act", lambda e, i=i, ps_i=ps_i: e.activation(B[:, SC + i, :Tt], pst[ps_i][:, :Tt], AF.Copy),
                     reads=[PSr[ps_i]], writes=[Br[SC + i]])
            sc = 128.0 ** -0.5
            HB = c.HB
            NW = HB * 128
            for g in range(c.NKV):
                for blk in range(nb):
                    for half in range(c.GQ // HB):
                        h0 = g * c.GQ + half * HB
                        qap = B[:, h0:h0 + HB, blk * 128:(blk + 1) * 128]
                        qregs = Br[h0:h0 + HB]
                        ets = []
                        for which in range(2):
                            kap = kT[:, g, blk * 128 + which * 128: blk * 128 + which * 128 + 128]
                            kreg = [Kprev[g], Kcur[g]] if blk == 0 and which == 0 else [Kcur[g]]
                            ps_i = next_ps()
                            P.op("pe", lambda e, ps_i=ps_i, kap=kap, qap=qap: e.matmul(
                                pst[ps_i][:, :NW].rearrange("p (a b) -> p a b", b=128), kap, qap, start=True, stop=True),
                                reads=kreg + qregs, writes=[PSr[ps_i]])
                            ei = next_et()
                            P.op("act", lambda e, ps_i=ps_i, ei=ei: e.activation(ET[ei][:, :NW], pst[ps_i][:, :NW], AF.Exp, scale=sc),
                                 reads=[PSr[ps_i]], writes=[ETr[ei]])
                            if which == 1:
                                mcol = 0
                            else:
                                mcol = 256 if (first_real and blk == 0) else 128
                            mt = mk[:, mcol:mcol + 128]
                            mb = bass.AP(mt.tensor, mt.offset, [mt.ap[0], [0, HB], mt.ap[1]])
                            P.op("dve", lambda e, ei=ei, mb=mb: e.tensor_tensor(
                                out=ET[ei][:, :NW].rearrange("p (a b) -> p a b", b=128),
                                in0=ET[ei][:, :NW].rearrange("p (a b) -> p a b", b=128), in1=mb, op=ALU.mult),
                                reads=[ETr[ei], CSr], writes=[ETr[ei]])
                            ets.append(ei)
                        pd = next_ps()
                        mm_group(pd, pst[pd][:, :NW], [(ones[:, :], ET[ei][:, :NW]) for ei in ets],
                                 reads=[ETr[ei] for ei in ets] + [CSr])
                        po = next_ps()
                        mm_group(po, pst[po][:, :NW],
                                 [(vtm[:, blk + which, g * 128:(g + 1) * 128], ET[ets[which]][:, :NW]) for which in range(2)],
                                 reads=[ETr[ei] for ei in ets] + [Vr[blk], Vr[blk + 1]])
                        fi = next_ft()
                        st = esink[:, h0:h0 + HB]
                        sbc = bass.AP(st.tensor, st.offset, [st.ap[0], st.ap[1], [0, 128]])
                        P.op("dve", lambda e, pd=pd, fi=fi, sbc=sbc: e.tensor_tensor(
                            out=FT[fi][:, :NW].rearrange("p (a b) -> p a b", b=128),
                            in0=pst[pd][:, :NW].rearrange("p (a b) -> p a b", b=128), in1=sbc, op=ALU.add),
                            reads=[PSr[pd], CSr], writes=[FTr[fi]])
                        P.op("dve", lambda e, fi=fi: e.reciprocal(FT[fi][:, :NW], FT[fi][:, :NW]),
                             reads=[FTr[fi]], writes=[FTr[fi]])
                        P.op("dve", lambda e, po=po, fi=fi, h0=h0, blk=blk: e.tensor_tensor(
                            out=A[:, h0:h0 + HB, blk * 128:(blk + 1) * 128],
                            in0=pst[po][:, :NW].rearrange("p (a b) -> p a b", b=128),
                            in1=FT[fi][:, :NW].rearrange("p (a b) -> p a b", b=128), op=ALU.mult),
                            reads=[PSr[po], FTr[fi]], writes=Ar[h0:h0 + HB])
        if SUB1 <= 3:
            return
        for g in range(c.NKV):
            P.op("dve", lambda e, g=g: e.tensor_copy(kT[:, g, 0:128], kT[:, g, Tt:Tt + 128]),
                 reads=[Kcur[g]], writes=[Kprev[g]])
        P.op("dve", lambda e: e.tensor_copy(vtm[:, 0, :], vtm[:, nb, :]), reads=[Vr[nb]], writes=[Vr[0]])
        if not halo:
            cross_attn(1, Tt)
            out_proj("mix1", c.U1, Tt)

    cst_inv = sb("cst_inv", [128, c.NG * 16], F32)
    P.op("sp", lambda e: e.dma_start(out=cst[:, :], in_=consts), writes=[CSr], chan="cst")
    CIr = Region()
    P.op("sp", lambda e: e.dma_start(out=cst_inv[:, :], in_=invf), writes=[CIr], chan="cst2")
    P.op("dve", lambda e: e.tensor_copy(mk[:, :], cst[:, c.cMK:c.cMK + 384]), reads=[CSr], writes=[CSr])
    P.op("dve", lambda e: e.tensor_copy(rm[:, :], cst[:, c.cRM:c.cRM + 128]), reads=[CSr], writes=[CSr])
    P.op("dve", lambda e: e.memset(ones[:, :], 1.0), writes=[CSr])
    P.op("dve", lambda e: e.memset(epst[:, :], EPS), writes=[CSr])
    P.op("dve", lambda e: e.memset(uhist[:, :, :], 0.0), writes=UHr)
    P.op("dve", lambda e: e.memset(kT[:, :, :], 0.0), writes=Kprev + Kcur)
    P.op("dve", lambda e: e.memset(vtm[:, :, :], 0.0), writes=Vr)
    P.op("act", lambda e: e.activation(esink[:, :], cst[:, c.cSK:c.cSK + c.NQ], AF.Exp), reads=[CSr], writes=[CSr])
    STAGE = getattr(c, "STAGE", 99)
    for q4 in range(4 if STAGE >= 2 else 0):
        q = DC // 4
        P.op("sp", lambda e, q4=q4, q=q: e.dma_start(
            out=h[:, q4 * q:(q4 + 1) * q, :c.MEM],
            in_=memT[q4 * q * 128:(q4 + 1) * q * 128, :].rearrange("(c p) t -> p c t", p=128)),
            writes=Hr[q4 * q:(q4 + 1) * q], chan="x%d" % q4)
    for layer in range(2 if STAGE >= 2 else 0):
        rmsnorm_to_A(c.cG + (2 + layer) * DC, c.MEM)
        for j in range(XC):
            ps_i = proj_unit("kv", layer * 2 * XC + j, c.MEM)
            P.op("act", lambda e, j=j, ps_i=ps_i, layer=layer: e.activation(KmT[:, layer * XC + j, :], pst[ps_i][:, :c.MEM], AF.Copy),
                 reads=[PSr[ps_i]], writes=[KmR])
        for j in range(XC):
            ap, regs = load_unit("kv", layer * 2 * XC + XC + j)
            u3 = unit3(ap)
            ps_i = next_ps()
            for mc in range(c.MC):
                mm_group(ps_i, pst[ps_i][:, mc * 128:(mc + 1) * 128],
                         [(A[:, kc, mc * 128:(mc + 1) * 128], u3[:, kc, :]) for kc in range(DC)],
                         reads=regs + Ar)
            P.op("act", lambda e, j=j, ps_i=ps_i, layer=layer: e.activation(
                Vm[:, layer * c.MC:(layer + 1) * c.MC, j * 128:(j + 1) * 128],
                pst[ps_i][:, :c.MC * 128].rearrange("p (b m) -> p b m", m=128), AF.Copy),
                reads=[PSr[ps_i]], writes=[VmR])

    tiles = [(0, c.HALO, True)] + [(c.HALO + i * T, T, False) for i in range(c.NTILE)]
    for ti, (t0, Tt, halo) in enumerate(tiles if STAGE >= 3 else []):
        first_real = (ti == 1)
        for q4 in range(4):
            q = DC // 4
            P.op("pool", lambda e, q4=q4, q=q, t0=t0, Tt=Tt: e.dma_start(
                out=h[:, q4 * q:(q4 + 1) * q, :Tt],
                in_=xT[q4 * q * 128:(q4 + 1) * q * 128, t0:t0 + Tt].rearrange("(c p) t -> p c t", p=128)),
                writes=Hr[q4 * q:(q4 + 1) * q], chan="x%d" % q4)
        DBG = getattr(c, "DBG", -1)

        def dump_h():
            o0 = t0 - c.HALO
            for q4 in range(4):
                q = DC // 4
                P.op("pool", lambda e, q4=q4, q=q, o0=o0, Tt=Tt: e.dma_start(
                    out=outT[q4 * q * 128:(q4 + 1) * q * 128, o0:o0 + Tt].rearrange("(c p) t -> p c t", p=128),
                    in_=h[:, q4 * q:(q4 + 1) * q, :Tt]),
                    reads=Hr[q4 * q:(q4 + 1) * q], chan="out%d" % q4)
        if DBG == 0 and not halo:
            dump_h(); continue
        mixer0(Tt, first_real)
        if DBG == 1 and not halo:
            dump_h(); continue
        if STAGE == 3:
            break
        mlp(0, Tt)
        if STAGE == 4:
            if getattr(c, "ROTV", 0) == 9:
                mlp(0, Tt)
            break
        if DBG == 2 and not halo:
            dump_h(); continue
        mixer1(Tt, t0, halo, first_real)
        if DBG == 3 and not halo:
            dump_h(); continue
        if STAGE == 5:
            break
        if halo:
            continue
        mlp(1, Tt)
        ps_i = next_ps()
        for cc in range(DC):
            si = next_sq()
            P.op("act", lambda e, cc=cc, si=si: e.activation(SQ[si][:, :Tt], h[:, cc, :Tt], AF.Square),
                 reads=[Hr[cc]], writes=[SQr[si]])
            P.op("pe", lambda e, cc=cc, si=si, ps_i=ps_i: e.matmul(pst[ps_i][:, :Tt], ones[:, :], SQ[si][:, :Tt],
                                                                  start=(cc == 0), stop=(cc == DC - 1)),
                 reads=[SQr[si], CSr], writes=[PSr[ps_i]])
        f1 = next_ft()
        P.op("act", lambda e, f1=f1, ps_i=ps_i: e.activation(FT[f1][:, :Tt], pst[ps_i][:, :Tt], AF.Sqrt,
                                                             bias=epst[:, 0:1], scale=1.0 / D),
             reads=[PSr[ps_i], CSr], writes=[FTr[f1]])
        f2 = next_ft()
        P.op("dve", lambda e, f1=f1, f2=f2: e.reciprocal(FT[f2][:, :Tt], FT[f1][:, :Tt]), reads=[FTr[f1]], writes=[FTr[f2]])
        gcol = c.cG + 6 * DC
        for cc in range(DC):
            P.op("dve", lambda e, cc=cc, f2=f2: e.scalar_tensor_tensor(
                out=h[:, cc, :Tt], in0=h[:, cc, :Tt], scalar=cst[:, gcol + cc:gcol + cc + 1],
                in1=FT[f2][:, :Tt], op0=ALU.mult, op1=ALU.mult),
                reads=[Hr[cc], FTr[f2], CSr], writes=[Hr[cc]])
        o0 = t0 - c.HALO
        for q4 in range(4):
            q = DC // 4
            P.op("pool", lambda e, q4=q4, q=q, o0=o0, Tt=Tt: e.dma_start(
                out=outT[q4 * q * 128:(q4 + 1) * q * 128, o0:o0 + Tt].rearrange("(c p) t -> p c t", p=128),
                in_=h[:, q4 * q:(q4 + 1) * q, :Tt]),
                reads=Hr[q4 * q:(q4 + 1) * q], chan="out%d" % q4)

    P.finalize(nc, es)
    es.close()
    return nc


def _units_in(W):
    K, N = W.shape
    KC, U = K // 128, N // 128
    return W.reshape(KC, 128, U, 128).transpose(2, 1, 0, 3).reshape(U, 128, KC * 128)


def _pad_units(u, n, width):
    U, p, w = u.shape
    out = np.zeros((n, 128, width), np.float32)
    out[:U, :, :w] = u
    return out


def _struct_consts(c):
    inv_freq = (ROPE_THETA ** (-np.arange(0, 32, 2, dtype=np.float32) / np.float32(32))).astype(np.float32)
    col = np.zeros((128,), np.float32)
    col[:32] = np.concatenate([inv_freq, inv_freq])
    k = np.arange(128)[:, None]
    q = np.arange(128)[None, :]
    m_cur = (k <= q).astype(np.float32)
    m_prev = (k > q).astype(np.float32)
    rmat = np.zeros((128, 128), np.float32)
    for m in range(16):
        rmat[m + 16, m] = -1.0
        rmat[m, m + 16] = 1.0
    return col, m_cur, m_prev, rmat


def prep_inputs(c, inp):
    f = lambda a: np.asarray(a, dtype=np.float32)
    x, mem, positions = f(inp["x"]), f(inp["mem"]), np.asarray(inp["positions"]).astype(np.int32)
    D, DC = c.D, c.DC
    col, m_cur, m_prev, rmat = _struct_consts(c)
    wkv = f(inp["w_mem_kv"])
    kv_units = np.concatenate([_units_in(wkv[0]), _units_in(wkv[1])], 0)
    wg = f(inp["pool_w_group"])[0]
    wg_units = np.concatenate([_pad_units(_units_in(wg[g]), c.GC, D) for g in range(c.NG)], 0)
    mix0 = np.concatenate([_units_in(f(inp["pool_w_in"])[0]), wg_units, _units_in(f(inp["pool_w_out"])[0])], 0)
    mix1 = np.concatenate([_units_in(f(inp["attn_w_in"])[0]), _units_in(f(inp["attn_w_out"])[0])], 0)
    w1 = f(inp["mlp_w1"]); w2 = f(inp["mlp_w2"])
    pieces = {
        "kv": kv_units, "mix0": mix0, "mix1": mix1,
        "w1_0": _units_in(w1[0]), "w2_0": w2[0].reshape(c.FCH, 128, D),
        "w1_1": _units_in(w1[1]), "w2_1": w2[1].reshape(c.FCH, 128, D),
    }
    shards = {}
    for name, U in c.PIECES:
        u = pieces[name]
        if u.shape[0] != U:
            u = _pad_units(u, U, D)
        u = np.ascontiguousarray(u).reshape(U * 128, D)
        shards[name] = [u for i in range(NCORES)]
    def percol(v):
        return np.ascontiguousarray(v.reshape(-1, 128).T)
    gains = [percol(f(inp["norm_mix"])[0]), percol(f(inp["norm_mix"])[1]),
             percol(f(inp["norm_mem"])[0]), percol(f(inp["norm_mem"])[1]),
             percol(f(inp["norm_mlp"])[0]), percol(f(inp["norm_mlp"])[1]),
             percol(f(inp["final_norm"]))]
    ps = percol(f(inp["pool_scale"])[0])
    sink = np.broadcast_to(f(inp["attn_sink"])[0][None, :], (128, c.NQ))
    in_maps = []
    nq = NCORES // 2
    for core in range(NCORES):
        b, qd = core // nq, core % nq
        s0 = qd * c.TOK
        xs = np.zeros((c.NTOKH, D), np.float32)
        ps_ = np.zeros((c.NTOKH,), np.int32)
        if qd == 0:
            xs[c.HALO:] = x[b, s0:s0 + c.TOK]
            ps_[c.HALO:] = positions[b, s0:s0 + c.TOK]
            m_pf = np.zeros_like(m_prev)
        else:
            xs[:] = x[b, s0 - c.HALO:s0 + c.TOK]
            ps_[:] = positions[b, s0 - c.HALO:s0 + c.TOK]
            m_pf = m_prev
        consts = np.concatenate(gains + [ps, sink, col[:, None], m_cur, m_prev, m_pf, rmat], axis=1).astype(np.float32)
        assert consts.shape == (128, c.NCONST), consts.shape
        inv = np.zeros((c.NG, 16), np.float32)
        for g, w in enumerate(c.WINDOWS):
            inv[g, :] = np.float32(1.0) / np.float32(w)
            if qd == 0:
                n = np.minimum(np.arange(1, 17), w).astype(np.float32)
                inv[g, :] = np.float32(1.0) / n
        m = {
            "xT": np.ascontiguousarray(xs.T),
            "memT": np.ascontiguousarray(mem[b].T),
            "pos": np.ascontiguousarray(np.broadcast_to(ps_[None, :], (128, c.NTOKH))),
            "consts": np.ascontiguousarray(consts),
            "invf": np.ascontiguousarray(np.broadcast_to(inv.reshape(1, -1), (128, c.NG * 16))),
        }
        for name, _ in c.PIECES:
            m["w_" + name] = np.ascontiguousarray(shards[name][core])
        in_maps.append(m)
    return in_maps


def run(c, inp):
    nc = build(c)
    in_maps = prep_inputs(c, inp)
    res = run_bass_kernel_spmd(nc, in_maps, core_ids=list(range(NCORES)))
    nq = NCORES // 2
    B = np.asarray(inp["x"]).shape[0]
    out = np.zeros((B, nq * c.TOK, c.D), np.float32)
    for core in range(NCORES):
        b, qd = core // nq, core % nq
        out[b, qd * c.TOK:(qd + 1) * c.TOK, :] = np.asarray(res.results[core]["outT"]).T
    return out


def kernel(**inputs):
    return run(make_cfg(), inputs)
```
